# Optimizing a Trainium2 kernel written in Bass

```python
import math
import jax, jax.numpy as jnp
from jax import lax
import numpy as np

D_MODEL = 1024
BATCH = 8
SEQ = 4096
DEPTH = 4

GRID_W = 64
CTX_LEN = 256
N_MIXERS = 2
N_NA_LAYERS = (DEPTH + N_MIXERS - 1) // N_MIXERS
N_HY_LAYERS = DEPTH // N_MIXERS
N_HEADS = 16
HEAD_DIM = D_MODEL // N_HEADS
KH_MAX = 8
KW = 16
Q_BLOCK_W = 16
K_BLOCK_W = Q_BLOCK_W + KW
N_COL_BLOCKS = GRID_W // Q_BLOCK_W
HY_ORDER = 2
HY_EMB_DIM = 33
HY_FILTER_ORDER = 64
HY_N_SIN = 3
HY_FAST_DECAY = 0.3
HY_SLOW_DECAY = 1.5
HY_TARGET = 1e-2
LN_EPS = 1e-5
ALPHA = (2 * DEPTH) ** 0.25
BETA = (8 * DEPTH) ** -0.25
NEG_INF = -1e30

kernel_name = "hybrid_na_hyena_deepnorm_prefix"


def layer_norm(x, g, b):
    xf = x.astype(jnp.float32)
    mu = xf.mean(-1, keepdims=True)
    var = jnp.square(xf - mu).mean(-1, keepdims=True)
    return ((xf - mu) * lax.rsqrt(var + LN_EPS)).astype(x.dtype) * g + b


def _col_block_layout():
    q_cols = np.arange(GRID_W).reshape(N_COL_BLOCKS, Q_BLOCK_W)
    q_start = np.clip(q_cols - KW // 2, 0, GRID_W - KW)
    blk_start = np.clip(np.arange(N_COL_BLOCKS) * Q_BLOCK_W - KW // 2, 0, GRID_W - K_BLOCK_W)
    k_cols = blk_start[:, None] + np.arange(K_BLOCK_W)
    kc = k_cols[:, None, :]
    in_win = (kc >= q_start[:, :, None]) & (kc < q_start[:, :, None] + KW)
    dcol = np.clip(kc - q_cols[:, :, None] + KW - 1, 0, 2 * KW - 2)
    return k_cols, in_win, dcol


def neighbourhood_attention(q, k, v, k_ctx, v_ctx, rpb):
    B, S, H, Dh = q.shape
    rows = S // GRID_W
    kh = min(KH_MAX, rows)
    k_cols, in_win, dcol = _col_block_layout()
    k_grid = k.reshape(B, rows, GRID_W, H, Dh)
    v_grid = v.reshape(B, rows, GRID_W, H, Dh)
    q_rows = jnp.moveaxis(q.reshape(B, rows, N_COL_BLOCKS, Q_BLOCK_W, H, Dh), 1, 0) * (Dh ** -0.5)
    mask = jnp.asarray(in_win)[None, None, :, :, None, :]
    n_lat = kh * K_BLOCK_W

    def one_row(args):
        r, q_r = args
        r0 = jnp.clip(r - kh // 2, 0, rows - kh)
        k_r = lax.dynamic_slice_in_dim(k_grid, r0, kh, axis=1)[:, :, k_cols]
        v_r = lax.dynamic_slice_in_dim(v_grid, r0, kh, axis=1)[:, :, k_cols]
        drow = r0 + jnp.arange(kh) - r + KH_MAX - 1
        bias = jnp.take(rpb, drow, axis=1)[:, :, dcol]
        bias = jnp.transpose(bias, (0, 2, 3, 1, 4)).astype(jnp.float32)
        s_lat = jnp.einsum('bnqhd,binjhd->bhnqij', q_r, k_r).astype(jnp.float32) + bias[None]
        s_lat = jnp.where(mask, s_lat, NEG_INF).reshape(B, H, N_COL_BLOCKS, Q_BLOCK_W, n_lat)
        s_ctx = jnp.einsum('bnqhd,bchd->bhnqc', q_r, k_ctx).astype(jnp.float32)
        p = jax.nn.softmax(jnp.concatenate([s_lat, s_ctx], axis=-1), axis=-1).astype(v.dtype)
        p_lat = p[..., :n_lat].reshape(B, H, N_COL_BLOCKS, Q_BLOCK_W, kh, K_BLOCK_W)
        p_ctx = p[..., n_lat:]
        o = (jnp.einsum('bhnqij,binjhd->bnqhd', p_lat, v_r)
             + jnp.einsum('bhnqc,bchd->bnqhd', p_ctx, v_ctx))
        return o.reshape(B, GRID_W, H, Dh)

    out = lax.map(one_row, (jnp.arange(rows), q_rows))
    return jnp.moveaxis(out, 0, 1).reshape(B, S, H, Dh)


def dense_attention(q, k, v):
    s = jnp.einsum('bqhd,bkhd->bhqk', q * (q.shape[-1] ** -0.5), k).astype(jnp.float32)
    p = jax.nn.softmax(s, axis=-1).astype(v.dtype)
    return jnp.einsum('bhqk,bkhd->bqhd', p, v)


def _heads(t):
    return t.reshape(t.shape[0], t.shape[1], N_HEADS, HEAD_DIM)


def na_mixer(h, hc, w_in, w_out, rpb, with_ctx_out):
    B, S, D = h.shape
    q, k, v, z = jnp.split(h @ w_in, 4, axis=-1)
    if with_ctx_out:
        qc, kc, vc, zc = jnp.split(hc @ w_in, 4, axis=-1)
    else:
        kc, vc = jnp.split(hc @ w_in[:, D:3 * D], 2, axis=-1)
    o = neighbourhood_attention(_heads(q), _heads(k), _heads(v), _heads(kc), _heads(vc), rpb).reshape(B, S, D)
    y = (o * jax.nn.silu(z)) @ w_out
    if not with_ctx_out:
        return y, None
    oc = dense_attention(_heads(qc), _heads(kc), _heads(vc)).reshape(hc.shape)
    yc = (oc * jax.nn.silu(zc)) @ w_out
    return y, yc


def short_conv(u, w, b):
    up = jnp.pad(u, ((0, 0), (1, 1), (0, 0)))
    return up[:, :-2] * w[0] + up[:, 1:-1] * w[1] + up[:, 2:] * w[2] + b


def hyena_filters(L, w1, b1, w2, b2, w3, b3, w4, freq):
    t = jnp.linspace(0.0, 1.0, L, dtype=jnp.float32)[:, None]
    bands = (HY_EMB_DIM - 1) // 2
    wpos = 2.0 * math.pi * jnp.arange(L, dtype=jnp.float32)[:, None] / L
    f = jnp.linspace(1e-4, bands - 1, bands, dtype=jnp.float32)[None, :]
    z = jnp.concatenate([t, jnp.cos(f * wpos), -jnp.sin(f * wpos)], axis=-1)
    a = jnp.sin(freq[0] * (z @ w1 + b1))
    a = jnp.sin(freq[1] * (a @ w2 + b2))
    a = jnp.sin(freq[2] * (a @ w3 + b3))
    width = w4.shape[-1] // (2 * HY_ORDER)
    hf = (a @ w4).astype(jnp.float32).reshape(L, HY_ORDER, 2, width)
    max_decay = math.log(HY_TARGET) / HY_FAST_DECAY
    min_decay = math.log(HY_TARGET) / HY_SLOW_DECAY
    deltas = jnp.abs(jnp.linspace(min_decay, max_decay, width, dtype=jnp.float32))
    hf = hf * jnp.exp(-t[:, :, None, None] * deltas)
    fwd = hf[:, :, 0]
    bwd = hf[1:, :, 1][::-1]
    taps = jnp.concatenate([fwd, jnp.zeros((1,) + fwd.shape[1:], jnp.float32), bwd], axis=0)
    taps = taps / jnp.sum(jnp.abs(taps), axis=0, keepdims=True)
    return jnp.fft.rfft(taps, n=2 * L, axis=0)


def long_conv(u, h_fft, skip):
    L = u.shape[1]
    uf = jnp.fft.rfft(u.astype(jnp.float32), n=2 * L, axis=1)
    y = jnp.fft.irfft(uf * h_fft[None], n=2 * L, axis=1)[:, :L]
    return (y + u.astype(jnp.float32) * skip).astype(u.dtype)


def hyena_mixer(h, w_in, w_out, conv_w, conv_b, filt, skip):
    L = h.shape[1]
    proj = h @ w_in
    width = proj.shape[-1] // 4
    pre, g = proj[..., :3 * width], proj[..., 3 * width:]
    v, x1, x2 = jnp.split(short_conv(pre, conv_w, conv_b), 3, axis=-1)
    h_fft = hyena_filters(L, *filt)
    z = x1 * long_conv(v, h_fft[:, 0], skip[0])
    z = x2 * long_conv(z, h_fft[:, 1], skip[1])
    return (z * jax.nn.silu(g)) @ w_out


def setup_inputs(seed: int = 0) -> dict:
    key = jax.random.key(seed)
    ks = jax.random.split(key, 22)
    D = D_MODEL
    f32 = jnp.float32

    def nrm(k, shape, s):
        return jax.random.normal(k, shape, f32) * s

    return {
        "x": nrm(ks[0], (BATCH, SEQ, D), 1.0),
        "c": nrm(ks[1], (BATCH, D), 1.0),
        "ctx": nrm(ks[2], (BATCH, CTX_LEN, D), 1.0),
        "c_ctx": nrm(ks[3], (D,), 1.0),
        "w_ada": nrm(ks[4], (DEPTH, D, 3 * D), 0.5 * D ** -0.5),
        "b_ada": nrm(ks[5], (DEPTH, 3 * D), 0.02),
        "w_in": nrm(ks[6], (DEPTH, D, 4 * D), D ** -0.5),
        "w_out": nrm(ks[7], (DEPTH, D, D), BETA * D ** -0.5),
        "ln_g": 1.0 + nrm(ks[8], (DEPTH, D), 0.05),
        "ln_b": nrm(ks[9], (DEPTH, D), 0.02),
        "na_rpb": nrm(ks[10], (N_NA_LAYERS, N_HEADS, 2 * KH_MAX - 1, 2 * KW - 1), 0.1),
        "hy_conv_w": nrm(ks[11], (N_HY_LAYERS, 3, 3 * D), 3 ** -0.5),
        "hy_conv_b": nrm(ks[12], (N_HY_LAYERS, 3 * D), 0.02),
        "hy_f_w1": nrm(ks[13], (N_HY_LAYERS, HY_EMB_DIM, HY_FILTER_ORDER), HY_EMB_DIM ** -0.5),
        "hy_f_b1": nrm(ks[14], (N_HY_LAYERS, HY_FILTER_ORDER), 0.1),
        "hy_f_w2": nrm(ks[15], (N_HY_LAYERS, HY_FILTER_ORDER, HY_FILTER_ORDER), HY_FILTER_ORDER ** -0.5),
        "hy_f_b2": nrm(ks[16], (N_HY_LAYERS, HY_FILTER_ORDER), 0.1),
        "hy_f_w3": nrm(ks[17], (N_HY_LAYERS, HY_FILTER_ORDER, HY_FILTER_ORDER), HY_FILTER_ORDER ** -0.5),
        "hy_f_b3": nrm(ks[18], (N_HY_LAYERS, HY_FILTER_ORDER), 0.1),
        "hy_f_w4": nrm(ks[19], (N_HY_LAYERS, HY_FILTER_ORDER, 2 * HY_ORDER * D), HY_FILTER_ORDER ** -0.5),
        "hy_f_freq": 1.0 + nrm(ks[20], (N_HY_LAYERS, HY_N_SIN, HY_FILTER_ORDER), 0.05),
        "hy_skip": nrm(ks[21], (N_HY_LAYERS, HY_ORDER, D), 0.5),
    }


def reference(x, c, ctx, c_ctx, w_ada, b_ada, w_in, w_out, ln_g, ln_b, na_rpb, hy_conv_w, hy_conv_b,
              hy_f_w1, hy_f_b1, hy_f_w2, hy_f_b2, hy_f_w3, hy_f_b3, hy_f_w4, hy_f_freq, hy_skip):
    sc = jax.nn.silu(c)
    sc_ctx = jax.nn.silu(c_ctx)
    for i in range(DEPTH):
        last = i == DEPTH - 1
        j = i // N_MIXERS
        shift, scale, gate = jnp.split((sc @ w_ada[i] + b_ada[i])[:, None, :], 3, axis=-1)
        shift_c, scale_c, gate_c = jnp.split(sc_ctx @ w_ada[i] + b_ada[i], 3, axis=-1)
        h = x * (1 + scale) + shift
        hc = ctx * (1 + scale_c) + shift_c
        if i % N_MIXERS == 0:
            y, yc = na_mixer(h, hc, w_in[i], w_out[i], na_rpb[j], not last)
        else:
            filt = (hy_f_w1[j], hy_f_b1[j], hy_f_w2[j], hy_f_b2[j], hy_f_w3[j], hy_f_b3[j],
                    hy_f_w4[j], hy_f_freq[j])
            y = hyena_mixer(h, w_in[i], w_out[i], hy_conv_w[j], hy_conv_b[j], filt, hy_skip[j])
            yc = None if last else hyena_mixer(hc, w_in[i], w_out[i], hy_conv_w[j], hy_conv_b[j], filt, hy_skip[j])
        x = layer_norm(ALPHA * x + gate * y, ln_g[i], ln_b[i])
        if not last:
            ctx = layer_norm(ALPHA * ctx + gate_c * yc, ln_g[i], ln_b[i])
    return x
```

```python
import numpy as np
import concourse.bass as bass
import concourse.mybir as mybir
from concourse.bass_utils import run_bass_kernel_spmd

F32 = mybir.dt.float32
BF16 = mybir.dt.bfloat16
AF = mybir.ActivationFunctionType
ALU = mybir.AluOpType
AX = mybir.AxisListType


LOAD_Q = "act"


class Res:
    __slots__ = ("name", "ap", "last_w", "reads", "sem", "semcnt", "excl")

    def __init__(self, name, ap=None):
        self.name = name
        self.ap = ap
        self.last_w = {}
        self.reads = {}
        self.sem = None
        self.semcnt = 0
        self.excl = False


class KB:
    def __init__(self, nc):
        self.nc = nc
        self.eng = {"pe": nc.tensor, "act": nc.scalar, "dve": nc.vector, "pool": nc.gpsimd, "sp": nc.sync}
        self.esem = {}
        self.ecnt = {}
        self.waited = {}
        for k_ in self.eng:
            self.esem[k_] = nc.alloc_semaphore(name="es_" + k_)
            self.ecnt[k_] = 0
            self.waited[k_] = {}
        self.n_inst = 0
        self.out_events = {}
        self.all_dma = {}

    def _uniq(self, name):
        self.uid = getattr(self, "uid", 0) + 1
        return "%s_u%d" % (name, self.uid)

    def sb(self, name, shape, dtype):
        name = self._uniq(name)
        if getattr(self, "layer_stack", None) is not None:
            t = self.layer_stack.enter_context(self.nc.sbuf_tensor(name, list(shape), dtype))
            return Res(name, t.ap() if hasattr(t, "ap") and callable(t.ap) else t)
        t = self.nc.alloc_sbuf_tensor(name, list(shape), dtype)
        return Res(name, t.ap() if hasattr(t, "ap") and callable(t.ap) else t)

    def sbt(self, name, shape, dtype):
        name = self._uniq(name)
        t = self.stack.enter_context(self.nc.sbuf_tensor(name, list(shape), dtype))
        return Res(name, t.ap() if hasattr(t, "ap") and callable(t.ap) else t)

    def ps(self, name, shape, dtype):
        t = self.nc.alloc_psum_tensor(name, list(shape), dtype)
        rr = Res(name, t.ap() if hasattr(t, "ap") and callable(t.ap) else t)
        rr.excl = True
        return rr

    def dram_res(self, name):
        return Res(name)

    def _wait_deps(self, ek, reads, writes):
        deps = {}

        def add(d):
            for sid, (sem, val) in d.items():
                if sid not in deps or deps[sid][1] < val:
                    deps[sid] = (sem, val)

        for r in reads:
            add(r.last_w)
        for w_ in writes:
            add(w_.last_w)
            add(w_.reads)
        eng = self.eng[ek]
        wd = self.waited[ek]
        own = id(self.esem[ek])
        for sid, (sem, val) in deps.items():
            if ek == "pe" and sid == own:
                continue
            if wd.get(sid, 0) >= val:
                continue
            eng.wait_ge(sem, val)
            wd[sid] = val

    def _record(self, ev, reads, writes):
        sid = id(ev[0])
        for r in reads:
            cur = r.reads.get(sid)
            if cur is None or cur[1] < ev[1]:
                r.reads[sid] = ev
        for w_ in writes:
            w_.last_w = {sid: ev}
            w_.reads = {}

    def op(self, ek, fn, r=(), w=()):
        w = list(w) + [x for x in r if x.excl and x not in w]
        r = [x for x in r if not x.excl]
        self._wait_deps(ek, r, w)
        inst = fn(self.eng[ek])
        self.ecnt[ek] += 1
        inst.then_inc(self.esem[ek], 1)
        ev = (self.esem[ek], self.ecnt[ek])
        self._record(ev, r, w)
        self.n_inst += 1
        return inst

    def dma(self, ek, out_ap, in_ap, r=(), w=(), sem_res=None, is_output=False, **kw):
        if ek == "sp" and any(x.ap is not None for x in w):
            ek = LOAD_Q
        self._wait_deps(ek, r, w)
        if sem_res is None:
            cands = [x for x in list(w) + list(r) if x.ap is not None]
            sem_res = cands[0] if cands else (list(w) + list(r))[0]
        if sem_res.sem is None:
            pool = getattr(self, "sem_pool", None)
            if pool:
                sem_res.sem, sem_res.semcnt = pool.pop()
            else:
                self.nsem = getattr(self, "nsem", 0) + 1
                sem_res.sem = self.nc.alloc_semaphore(name="ds%d_%s" % (self.nsem, sem_res.name))
            if not hasattr(self, "sem_live"):
                self.sem_live = []
            self.sem_live.append(sem_res)
        inst = self.eng[ek].dma_start(out=out_ap, in_=in_ap, **kw)
        sem_res.semcnt += 16
        inst.then_inc(sem_res.sem, 16)
        ev = (sem_res.sem, sem_res.semcnt)
        self._record(ev, r, w)
        self.all_dma[id(sem_res.sem)] = ev
        if is_output:
            self.out_events[id(sem_res.sem)] = ev
        self.n_inst += 1
        return inst

    def recycle(self):
        if not hasattr(self, "sem_pool"):
            self.sem_pool = []
        for r_ in getattr(self, "sem_live", []):
            self.sem_pool.append((r_.sem, r_.semcnt))
            r_.sem = None
        self.sem_live = []

    def finish(self):
        for sid, (sem, val) in self.out_events.items():
            self.eng["sp"].wait_ge(sem, val)


D = 1024
S = 4096
NT = 32
H = 16
DH = 64
CTX = 256
GW = 64
KH = 8
KW = 16
ALPHA = (2 * 4) ** 0.25
LN_EPS = 1e-5
NSLOT = 22
NEG = -1e30


def _r0(r):
    return min(max(r - KH // 2, 0), GW - KH)


def na_bias_index():
    sent = H * 15 * 31
    idx = np.full((64, H, NSLOT, 64), sent, dtype=np.int64)
    slot_delta = {}
    for s in range(1, 16):
        slot_delta[s] = s - 8
    slot_delta[18] = -4
    slot_delta[19] = 3
    qc = np.arange(64)[:, None]
    kc = np.arange(64)[None, :]
    qs = np.clip(qc - KW // 2, 0, GW - KW)
    inwin = (kc >= qs) & (kc < qs + KW)
    dcol = np.clip(kc - qc + KW - 1, 0, 2 * KW - 2)
    for s, dl in slot_delta.items():
        drow = dl + 7
        for h in range(H):
            v = h * 15 * 31 + drow * 31 + dcol
            idx[:, h, s, :] = np.where(inwin, v, sent)
    return idx


def na_slot(r, kt):
    r0 = _r0(r)
    ka, kb = 2 * kt, 2 * kt + 1
    va = r0 <= ka <= r0 + KH - 1
    vb = r0 <= kb <= r0 + KH - 1
    if va and vb:
        return ka - r + 8
    if vb:
        assert kb - r == -4
        return 17
    if va:
        assert ka - r == 3
        return 19
    return 20


def na_key_tiles(T):
    rows = [2 * T, 2 * T + 1]
    lo = min(_r0(r) for r in rows) // 2
    hi = max(_r0(r) + KH - 1 for r in rows) // 2
    return list(range(lo, hi + 1))


def kb_barrier(k):
    evs = [(k.esem[e], k.ecnt[e]) for e in k.eng if k.ecnt[e] > 0]
    evs += list(k.all_dma.values())
    for e in k.eng:
        wd = k.waited[e]
        for sem, val in evs:
            if wd.get(id(sem), 0) < val:
                k.eng[e].wait_ge(sem, val)
                wd[id(sem)] = val


def emit_prep(k, nc, L, i, use_ctx_out, gate_sel=0, nbuf=1):
    P = L["P"]
    craw = k.sbt("craw", [128, 8, 2], F32)
    scT = k.sbt("scT", [128, 8, 2], F32)
    k.dma("sp", craw.ap[:, :, 0], L["c"].rearrange("(k p) -> p k", p=128), w=[craw], allow_slow_non_contiguous=True)
    k.dma("sp", craw.ap[:, :, 1], L["c_ctx"].rearrange("(k p) -> p k", p=128), w=[craw], allow_slow_non_contiguous=True)
    k.op("act", lambda e: e.activation(scT.ap[:].rearrange("p a b -> p (a b)"), craw.ap[:].rearrange("p a b -> p (a b)"), AF.Silu), r=[craw], w=[scT])
    ones = k.sbt("ones128", [128, 128], F32)
    k.op("dve", lambda e: e.memset(ones.ap[:], 1.0), w=[ones])
    scB = k.sbt("scB", [128, 2, 8, 128], F32)
    for j in range(2):
        for kk in range(8):
            k.op("dve", lambda e: e.tensor_scalar(scB.ap[:, j, kk, :], ones.ap[:], scT.ap[:, kk, j:j + 1], None, ALU.mult), r=[ones, scT], w=[scB])
    bF = k.sbt("bF", [128, 24, 2], F32)
    for j in range(2):
        k.dma("sp", bF.ap[:, :, j], L["b_ada"].rearrange("(m p) -> p m", p=128), w=[bF], allow_slow_non_contiguous=True)
    bG = k.sbt("bG", [128, 1024], F32)
    k.dma("sp", bG.ap[:], L["b_ada"][2048:3072].partition_broadcast(128), w=[bG])
    k.dma("sp", P["g_bc"].ap[:], L["ln_g"].partition_broadcast(128), w=[P["g_bc"]])
    k.dma("sp", P["b_bc"].ap[:], L["ln_b"].partition_broadcast(128), w=[P["b_bc"]])
    ps_mod = L["psum"][0]
    ps_g = L["psum"][1:5]
    was = [k.sbt("wa%d" % s_, [128, 3072], F32) for s_ in range(nbuf)]
    for kk in range(8):
        wa = was[kk % nbuf]
        k.dma("sp", wa.ap[:], L["w_ada"][kk * 128:(kk + 1) * 128, :], w=[wa])
        for m in range(24):
            k.op("pe", lambda e: e.matmul(ps_mod.ap[:, kk * 48 + 2 * m:kk * 48 + 2 * m + 2], wa.ap[:, m * 128:(m + 1) * 128], scT.ap[:, kk, :],
                                          start=True, stop=True), r=[wa, scT], w=[ps_mod])
        for j in range(2):
            for n in range(2):
                pg = ps_g[j * 2 + n]
                k.op("pe", lambda e: e.matmul(pg.ap[:], scB.ap[:, j, kk, :], wa.ap[:, 2048 + n * 512:2048 + (n + 1) * 512],
                                              start=(kk == 0), stop=(kk == 7)), r=[wa, scB], w=[pg])
    modF = P["modF"]
    k.op("dve", lambda e: e.tensor_tensor(modF.ap[:].rearrange("p a b -> p (a b)"), ps_mod.ap[:, 0:48], bF.ap[:].rearrange("p a b -> p (a b)"), ALU.add),
         r=[ps_mod, bF], w=[modF])
    for kk in range(1, 8):
        k.op("dve", lambda e: e.tensor_tensor(modF.ap[:].rearrange("p a b -> p (a b)"), ps_mod.ap[:, kk * 48:(kk + 1) * 48], modF.ap[:].rearrange("p a b -> p (a b)"), ALU.add),
             r=[ps_mod], w=[modF])
    k.op("dve", lambda e: e.tensor_scalar(modF.ap[:, 8:16, :], modF.ap[:, 8:16, :], 1.0, None, ALU.add), r=[modF], w=[modF])
    gates = [k.sbt("gate%d" % j, [128, 1024], F32) for j in range(2)]
    for j in range(2):
        for n in range(2):
            pg = ps_g[j * 2 + n]
            k.op("dve", lambda e: e.tensor_tensor(gates[j].ap[:, n * 512:(n + 1) * 512], pg.ap[:], bG.ap[:, n * 512:(n + 1) * 512], ALU.add),
                 r=[pg, bG], w=[gates[j]])
    wos = [k.sbt("wos%d" % s_, [128, 1024], F32) for s_ in range(nbuf)]
    for kk in range(8):
        wo = wos[kk % nbuf]
        k.dma("sp", wo.ap[:], L["w_out"][kk * 128:(kk + 1) * 128, :], w=[wo])
        k.op("dve", lambda e: e.tensor_tensor(P["woutg"].ap[:, kk, :], wo.ap[:], gates[gate_sel].ap[:], ALU.mult), r=[wo, gates[gate_sel]], w=[P["woutg"]])
        if use_ctx_out:
            k.op("pool", lambda e: e.tensor_tensor(P["woutgc"].ap[:, kk, :], wo.ap[:], gates[1].ap[:], ALU.mult), r=[wo, gates[1]], w=[P["woutgc"]])
    wis = [k.sbt("wis%d" % s_, [128, 1024], F32) for s_ in range(max(1, 2 * nbuf - 1))]
    q = 0
    for kk in range(8):
        for n in range(4):
            wi = wis[q % max(1, 2 * nbuf - 1)]
            k.dma("sp", wi.ap[:], L["w_in"][kk * 128:(kk + 1) * 128, n * 1024:(n + 1) * 1024], w=[wi])
            if q % 2 == 0:
                k.op("dve", lambda e: e.tensor_copy(P["win"].ap[:, kk, n * 1024:(n + 1) * 1024], wi.ap[:]), r=[wi], w=[P["win"]])
            else:
                k.op("pool", lambda e: e.tensor_copy(P["win"].ap[:, kk, n * 1024:(n + 1) * 1024], wi.ap[:]), r=[wi], w=[P["win"]])
            q += 1


def emit_proj(k, L, xt, which, mj, dst, after_tr=None):
    P = L["P"]
    T_ = L["T"]
    pt = L["ps_t"]
    for c in range(8):
        k.op("pe", lambda e: e.transpose(pt[c // 4].ap[:, (c % 4) * 128:(c % 4 + 1) * 128], xt.ap[:, c * 128:(c + 1) * 128], P["identf"].ap[:]),
             r=[xt, P["identf"]], w=[pt[c // 4]])
    if after_tr is not None:
        after_tr()
    hT = T_["hT"]
    for c in range(8):
        k.op("dve", lambda e: e.tensor_scalar(hT.ap[:, c, :], pt[c // 4].ap[:, (c % 4) * 128:(c % 4 + 1) * 128],
                                              P["modF"].ap[:, 8 + c, mj:mj + 1], P["modF"].ap[:, c, mj:mj + 1], ALU.mult, ALU.add),
             r=[pt[c // 4], P["modF"]], w=[hT])
    pp = L["ps_p"]
    win = P["win"]
    for letter in which:
        if letter in "qk":
            base = 0 if letter == "q" else 1024
            res, ap = dst[letter]
            for m in range(8):
                pb = pp[m // 4]
                for kk in range(8):
                    k.op("pe", lambda e: e.matmul(pb.ap[:, (m % 4) * 128:(m % 4 + 1) * 128], win.ap[:, kk, base + m * 128:base + (m + 1) * 128], hT.ap[:, kk, :],
                                                  start=(kk == 0), stop=(kk == 7)), r=[win, hT], w=[pb])
            for half in range(2):
                sc_ = 0.125 if letter == "q" else 1.0
                k.op("dve", lambda e: e.tensor_scalar(ap[:, half * 4:(half + 1) * 4, :], pp[half].ap[:].rearrange("p (a b) -> p a b", a=4), sc_, None, ALU.mult),
                     r=[pp[half]], w=[res])
        else:
            base = 2048 if letter == "v" else 3072
            res, ap = dst[letter]
            for n in range(2):
                pb = pp[n]
                for kk in range(8):
                    k.op("pe", lambda e: e.matmul(pb.ap[:], hT.ap[:, kk, :], win.ap[:, kk, base + n * 512:base + (n + 1) * 512],
                                                  start=(kk == 0), stop=(kk == 7)), r=[win, hT], w=[pb])
                if letter == "v":
                    k.op("dve", lambda e: e.tensor_copy(ap[:, n * 8:(n + 1) * 8, 0:64], pb.ap[:].rearrange("p (a b) -> p a b", a=8)), r=[pb], w=[res])
                else:
                    zt = T_["ztmp"]
                    k.op("dve", lambda e: e.tensor_copy(zt.ap[:], pb.ap[:]), r=[pb], w=[zt])
                    k.op("act", lambda e: e.activation(ap[:, n * 512:(n + 1) * 512], zt.ap[:], AF.Silu), r=[zt], w=[res])


def emit_attn(k, L, qres, keys, bias_rows, szres, xres, woutg, out_dram_ap, out_res, is_output):
    P = L["P"]
    T_ = L["T"]
    ps_s = L["ps_s"]
    ps_o = L["ps_o"]
    PT = T_["PT"]
    O = T_["O"]
    nk = len(keys)
    rec = T_["rec"]
    sbanks = [ps_s, L["ps_p"]]

    def stage_a(h):
        c = h // 2
        pb = 64 * (h % 2)
        bk = sbanks[h % 2]
        for i, (kres, kap, vres, vap, kt) in enumerate(keys):
            bank = bk[i // 4]
            col = (i % 4) * 128
            has_bias = (kt is not None) and not L.get("nobias", False)
            if not has_bias:
                k.op("pe", lambda e: e.matmul(bank.ap[:, col:col + 128], kap[pb:pb + 64, c, :], qres.ap[pb:pb + 64, c, :],
                                              start=True, stop=True), r=[kres, qres], w=[bank])
            else:
                tb = P["TB"]
                hb = pb
                for rp in range(2):
                    s0 = na_slot(bias_rows[rp], kt)
                    cc = col + 64 * rp
                    k.op("pe", lambda e: e.matmul(bank.ap[:, cc:cc + 64], kap[pb:pb + 64, c, :], qres.ap[pb:pb + 64, c, 64 * rp:64 * rp + 64],
                                                  start=True, stop=False), r=[kres, qres], w=[bank])
                    k.op("pe", lambda e: e.matmul(bank.ap[:, cc:cc + 64],
                                                  tb.ap[hb:hb + 64, h // 2, s0:s0 + 2, :].rearrange("p a b -> p (a b)"),
                                                  P["identb"].ap[hb:hb + 64, hb:hb + 64], start=False, stop=True),
                         r=[tb, P["identb"]], w=[bank])

    def stage_b(h):
        bk = sbanks[h % 2]
        pt_ = PT[h % 2]
        stp = T_["stmp"][h % 2]
        n0 = min(nk, 4) * 128
        k.op("dve", lambda e: e.tensor_copy(stp.ap[:, 0:n0], bk[0].ap[:, 0:n0]), r=[bk[0]], w=[stp])
        if nk > 4:
            n1 = (nk - 4) * 128
            k.op("dve", lambda e: e.tensor_copy(stp.ap[:, 512:512 + n1], bk[1].ap[:, 0:n1]), r=[bk[1]], w=[stp])
        k.op("act", lambda e: e.activation(pt_.ap[:, 0:nk * 128], stp.ap[:, 0:nk * 128], AF.Exp), r=[stp], w=[pt_])

    def stage_c(h):
        pt_ = PT[h % 2]
        po = ps_o[(h // 4) % 2]
        oc = (h % 4) * 128
        for i, (kres, kap, vres, vap, kt) in enumerate(keys):
            k.op("pe", lambda e: e.matmul(po.ap[:, oc:oc + 65], pt_.ap[:, i * 128:(i + 1) * 128], vap[:, h, :],
                                          start=(i == 0), stop=(i == nk - 1)), r=[pt_, vres], w=[po])
        if h % 4 == 3:
            k.op("dve", lambda e: e.reciprocal(rec.ap[:], po.ap[:].rearrange("p (a b) -> p a b", a=4)[:, :, 64]), r=[po], w=[rec])
            for q in range(4):
                hh = h - 3 + q
                k.op("dve", lambda e: e.tensor_scalar(O.ap[:, hh * 64:(hh + 1) * 64], po.ap[:, q * 128:q * 128 + 64], rec.ap[:, q:q + 1], None, ALU.mult),
                     r=[po, rec], w=[O])

    stage_a(0)
    for h in range(H):
        stage_b(h)
        if h + 1 < H:
            stage_a(h + 1)
        stage_c(h)
    k.op("pool", lambda e: e.tensor_tensor(O.ap[:], O.ap[:], szres.ap[:], ALU.mult), r=[szres], w=[O])
    pt = L["ps_t"]
    for c in range(8):
        k.op("pe", lambda e: e.transpose(pt[c // 4].ap[:, (c % 4) * 128:(c % 4 + 1) * 128], O.ap[:, c * 128:(c + 1) * 128], P["identf"].ap[:]),
             r=[O, P["identf"]], w=[pt[c // 4]])
    GT = T_["GT"]
    for half in range(2):
        k.op("dve", lambda e: e.tensor_copy(GT.ap[:, half * 4:(half + 1) * 4, :], pt[half].ap[:].rearrange("p (a b) -> p a b", a=4)), r=[pt[half]], w=[GT])
    pp = L["ps_p"]
    for n in range(2):
        for kk in range(8):
            k.op("pe", lambda e: e.matmul(pp[n].ap[:], GT.ap[:, kk, :], woutg.ap[:, kk, n * 512:(n + 1) * 512], start=(kk == 0), stop=(kk == 7)),
                 r=[GT, woutg], w=[pp[n]])
    emit_resid_ln(k, L, xres, pp, out_dram_ap, out_res, is_output)


def emit_resid_ln(k, L, xres, pp, out_dram_ap, out_res, is_output):
    P = L["P"]
    T_ = L["T"]
    for n in range(2):
        k.op("dve", lambda e: e.scalar_tensor_tensor(xres.ap[:, n * 512:(n + 1) * 512], xres.ap[:, n * 512:(n + 1) * 512], float(ALPHA), pp[n].ap[:], ALU.mult, ALU.add),
             r=[pp[n]], w=[xres])
    st = T_["st"]
    mv = T_["mv"]
    for n in range(2):
        k.op("dve", lambda e: e.bn_stats(st.ap[:, n, :], xres.ap[:, n * 512:(n + 1) * 512]), r=[xres], w=[st])
    k.op("dve", lambda e: e.bn_aggr(mv.ap[:], st.ap[:].rearrange("p a b -> p (a b)")), r=[st], w=[mv])
    import os
    if os.environ.get("HYDBG") != "nosqrt":
        if "dummy" in T_:
            k.op("act", lambda e: e.activation(T_["dummy"].ap[:], P["eps"].ap[:], AF.Exp), r=[P["eps"]], w=[T_["dummy"]])
        k.op("act", lambda e: e.activation(mv.ap[:, 1:2], mv.ap[:, 1:2], AF.Sqrt, bias=P["eps"].ap[:, 0:1]), r=[P["eps"]], w=[mv])
    k.op("dve", lambda e: e.reciprocal(mv.ap[:, 1:2], mv.ap[:, 1:2]), w=[mv])
    k.op("dve", lambda e: e.tensor_scalar(xres.ap[:], xres.ap[:], mv.ap[:, 0:1], mv.ap[:, 1:2], ALU.subtract, ALU.mult), r=[mv], w=[xres])
    k.op("pool", lambda e: e.tensor_tensor(xres.ap[:], xres.ap[:], P["g_bc"].ap[:], ALU.mult), r=[P["g_bc"]], w=[xres])
    k.op("pool", lambda e: e.tensor_tensor(xres.ap[:], xres.ap[:], P["b_bc"].ap[:], ALU.add), r=[P["b_bc"]], w=[xres])
    k.dma("sp", out_dram_ap, xres.ap[:], r=[xres], w=[out_res] if out_res is not None else [], sem_res=xres, is_output=is_output)


def build_na_layer(with_ctx_out, stop_after=None, ntiles=NT, env=None):
    if env is None:
        nc = bass.Bass("TRN2", target_bir_lowering=False)
        dt_in = lambda n, s: nc.dram_tensor(n, list(s), F32, kind="ExternalInput").ap()
    else:
        nc = env["nc"]
        dt_in = lambda n, s: env["aps"][n]
    L = {}
    import os
    L["nobias"] = os.environ.get("NOBIAS") == "1"
    x = dt_in("x", [S, D]); ctx = dt_in("ctx", [CTX, D])
    L["c"] = dt_in("c", [D]); L["c_ctx"] = dt_in("c_ctx", [D])
    L["w_ada"] = dt_in("w_ada", [D, 3 * D]); L["b_ada"] = dt_in("b_ada", [3 * D])
    L["w_in"] = dt_in("w_in", [D, 4 * D]); L["w_out"] = dt_in("w_out", [D, D])
    L["ln_g"] = dt_in("ln_g", [D]); L["ln_b"] = dt_in("ln_b", [D])
    tbd = dt_in("tb_d", [128, 8, NSLOT * 64])
    idd = dt_in("identf_d", [128, 128])
    if env is None:
        x_out = nc.dram_tensor("x_out", [S, D], F32, kind="ExternalOutput").ap()
        ctx_out = nc.dram_tensor("ctx_out", [CTX, D], F32, kind="ExternalOutput").ap()
        k = KB(nc)
    else:
        x_out = env["aps"]["x_out"]; ctx_out = env["aps"].get("ctx_out")
        k = env["k"]
    P = {}
    L["P"] = P
    P["identf"] = k.sb("identf", [128, 128], F32)
    P["identb"] = k.sb("identb", [128, 128], BF16)
    P["eps"] = k.sb("eps", [128, 1], F32)
    P["modF"] = k.sb("modF", [128, 24, 2], F32)
    P["g_bc"] = k.sb("g_bc", [128, D], F32)
    P["b_bc"] = k.sb("b_bc", [128, D], F32)
    P["woutg"] = k.sb("woutg", [128, 8, D], BF16)
    P["woutgc"] = k.sb("woutgc", [128, 8, D], BF16)
    P["win"] = k.sb("win", [128, 8, 4 * D], BF16)
    P["TB"] = k.sb("TB", [128, 8, NSLOT, 64], BF16)
    psum = env["psum"] if env is not None else [k.ps("psb%d" % i_, [128, 512], F32) for i_ in range(8)]
    L["psum"] = psum
    L["ps_t"] = psum[0:2]; L["ps_p"] = psum[2:4]; L["ps_s"] = psum[4:6]; L["ps_o"] = psum[6:8]
    k.dma("sp", P["identf"].ap[:], idd, w=[P["identf"]])
    k.op("dve", lambda e: e.tensor_copy(P["identb"].ap[:], P["identf"].ap[:]), r=[P["identf"]], w=[P["identb"]])
    k.op("dve", lambda e: e.memset(P["eps"].ap[:], LN_EPS), w=[P["eps"]])
    T_ = {}
    L["T"] = T_
    T_["hT"] = k.sb("hT", [128, 8, 128], BF16)
    T_["PT"] = [k.sb("PT%d" % i_, [128, 896], BF16) for i_ in range(2)]
    T_["O"] = k.sb("O", [128, D], F32)
    T_["GT"] = k.sb("GT", [128, 8, 128], BF16)
    T_["rec"] = k.sb("rec", [128, 4], F32)
    T_["ztmp"] = k.sb("ztmp", [128, 512], F32)
    T_["stmp"] = [k.sb("stmp%d" % i_, [128, 896], F32) for i_ in range(2)]
    T_["st"] = k.sb("st", [128, 2, 6], F32)
    T_["mv"] = k.sb("mv", [128, 2], F32)
    KcT = [k.sb("KcT%d" % i_, [128, 8, 128], BF16) for i_ in range(2)]
    Vc = [k.sb("Vc%d" % i_, [128, H, 65], BF16) for i_ in range(2)]
    for v_ in Vc:
        k.op("pool", lambda e: e.memset(v_.ap[:, :, 64:65], 1.0), w=[v_])
    from contextlib import ExitStack
    with ExitStack() as st1:
        k.stack = st1
        tbs = [k.sbt("tbs%d" % i_, [128, NSLOT * 64], F32) for i_ in range(2)]
        for hh in range(8):
            k.dma("sp", tbs[hh % 2].ap[:], tbd[:, hh, :], w=[tbs[hh % 2]])
            k.op("dve", lambda e: e.tensor_copy(P["TB"].ap[:, hh, :, :].rearrange("p a b -> p (a b)"), tbs[hh % 2].ap[:]), r=[tbs[hh % 2]], w=[P["TB"]])
        emit_prep(k, nc, L, 0, with_ctx_out)
        kb_barrier(k)
    if stop_after == 'prep':
        k.finish(); return nc, k
    with ExitStack() as st2:
        k.stack = st2
        QcT = [k.sbt("QcT%d" % i_, [128, 8, 128], BF16) for i_ in range(2)]
        SZc = [k.sbt("SZc%d" % i_, [128, D], BF16) for i_ in range(2)]
        xc = [k.sbt("xc%d" % i_, [128, D], F32) for i_ in range(2)]
        for t in range(2):
            k.dma("sp", xc[t].ap[:], ctx[t * 128:(t + 1) * 128, :], w=[xc[t]])
            dst = {"k": (KcT[t], KcT[t].ap), "v": (Vc[t], Vc[t].ap), "q": (QcT[t], QcT[t].ap), "z": (SZc[t], SZc[t].ap)}
            emit_proj(k, L, xc[t], "qkvz" if with_ctx_out else "kv", 1, dst)
        ckeys = [(KcT[t], KcT[t].ap, Vc[t], Vc[t].ap, None) for t in range(2)]
        if with_ctx_out:
            for t in range(2):
                emit_attn(k, L, QcT[t], ckeys, None, SZc[t], xc[t], P["woutgc"], ctx_out[t * 128:(t + 1) * 128, :], None, True)
        kb_barrier(k)
    if stop_after == 'ctx':
        k.finish(); return nc, k
    xa = [k.sb("xa%d" % i_, [128, D], F32) for i_ in range(1)]
    xr = [k.sb("xr%d" % i_, [128, D], F32) for i_ in range(2)]
    NQ = 4
    NR = 6
    QT = [k.sb("QT%d" % i_, [128, 8, 128], BF16) for i_ in range(NQ)]
    SZ = [k.sb("SZ%d" % i_, [128, D], BF16) for i_ in range(NQ)]
    KTr = [k.sb("KTr%d" % i_, [128, 8, 128], BF16) for i_ in range(NR)]
    Vr = [k.sb("Vr%d" % i_, [128, H, 65], BF16) for i_ in range(NR)]
    for v_ in Vr:
        k.op("pool", lambda e: e.memset(v_.ap[:, :, 64:65], 1.0), w=[v_])
    done = -1
    xa_holds = [-1]
    for T in range(ntiles):
        kts = na_key_tiles(T)
        while done < max(kts):
            done += 1
            xt = xa[0]
            if xa_holds[0] != done:
                k.dma("sp", xt.ap[:], x[done * 128:(done + 1) * 128, :], w=[xt])
                xa_holds[0] = done
            dst = {"q": (QT[done % NQ], QT[done % NQ].ap), "k": (KTr[done % NR], KTr[done % NR].ap),
                   "v": (Vr[done % NR], Vr[done % NR].ap), "z": (SZ[done % NQ], SZ[done % NQ].ap)}

            def prefetch(nxt=done + 1):
                if nxt < NT:
                    k.dma("sp", xa[0].ap[:], x[nxt * 128:(nxt + 1) * 128, :], w=[xa[0]])
                    xa_holds[0] = nxt
            emit_proj(k, L, xt, "qkvz", 0, dst, after_tr=prefetch)
        xres = xr[T % 2]
        k.dma("sp", xres.ap[:], x[T * 128:(T + 1) * 128, :], w=[xres])
        keys = [(KTr[kt % NR], KTr[kt % NR].ap, Vr[kt % NR], Vr[kt % NR].ap, kt) for kt in kts] + ckeys
        emit_attn(k, L, QT[T % NQ], keys, (2 * T, 2 * T + 1), SZ[T % NQ], xres, P["woutg"], x_out[T * 128:(T + 1) * 128, :], None, True)
    if env is not None:
        return nc, k
    k.finish()
    return nc, k


def na_table_host(rpb_j):
    flat = np.concatenate([np.asarray(rpb_j, np.float32).ravel(), np.array([NEG], np.float32)])
    tb = flat[na_bias_index()]
    tb = tb.reshape(64, 8, 2, NSLOT, 64).transpose(2, 0, 1, 3, 4).reshape(128, 8, NSLOT * 64)
    return np.ascontiguousarray(tb)


HY_N = 8192
HY_EMB = 33


def hy_tables(J=32):
    N = 256 * J
    Lh = 128 * J
    p = np.arange(128, dtype=np.float64)[:, None]
    k1 = np.arange(128, dtype=np.float64)[None, :]
    G = np.zeros((128, J, 2, 128), np.float32)
    G2 = np.zeros((128, J, 2, 128), np.float32)
    GT = np.zeros((128, J, 2, 128), np.float32)
    for j in range(J):
        th = 2 * np.pi * (J * p + j) * (k1 + 0.5) / N
        G[:, j, 0, :] = np.cos(th); G[:, j, 1, :] = -np.sin(th)
        th2 = 2 * np.pi * (J * (p + 128) + j) * (k1 + 0.5) / N
        G2[:, j, 0, :] = -np.cos(th2); G2[:, j, 1, :] = np.sin(th2)
        GT[:, j, 0, :] = (2.0 / N) * np.cos(th).T
        GT[:, j, 1, :] = (2.0 / N) * (-np.sin(th)).T
    jj = np.arange(J)[:, None]; k2 = np.arange(J)[None, :]
    Wr = np.cos(2 * np.pi * jj * k2 / J); Wi = -np.sin(2 * np.pi * jj * k2 / J)
    SA = np.zeros((2, 2, J, 2, 2, J)); SB = np.zeros_like(SA); SHA = np.zeros_like(SA); SHB = np.zeros_like(SA)
    SI = np.zeros((2, 2, J, 2, 2, J))
    for hh in range(2):
        for dup in range(2):
            SA[0, hh, :, dup, hh, :] = Wr; SA[1, hh, :, dup, hh, :] = -Wi
            SB[0, hh, :, dup, hh, :] = Wi; SB[1, hh, :, dup, hh, :] = Wr
        SHA[:, hh, :, 0, hh, :] = SA[:, hh, :, 0, hh, :]; SHA[:, hh, :, 1, hh, :] = SB[:, hh, :, 1, hh, :]
        SHB[:, hh, :, 0, hh, :] = -SB[:, hh, :, 0, hh, :]; SHB[:, hh, :, 1, hh, :] = SA[:, hh, :, 1, hh, :]
        SI[0, hh, :, 0, hh, :] = Wr.T; SI[1, hh, :, 0, hh, :] = Wi.T
        SI[0, hh, :, 1, hh, :] = -Wi.T; SI[1, hh, :, 1, hh, :] = Wr.T
    S5 = np.zeros((128, 5, 128), np.float32)
    for i_, m in enumerate((SA, SB, SHA, SHB, SI)):
        S5[:4 * J, i_, :4 * J] = m.reshape(4 * J, 4 * J)
    pos = np.zeros((128, 2 * J), np.int64)
    for j in range(J):
        pos[:, 2 * j] = J * np.arange(128) + j
        pos[:, 2 * j + 1] = Lh - J * np.arange(128) - j
    valid = pos <= Lh - 1
    posc = np.where(valid, pos, 0)
    t_all = np.linspace(0.0, 1.0, Lh, dtype=np.float32)
    wpos = (np.float32(2.0 * np.pi) * np.arange(Lh, dtype=np.float32) / np.float32(Lh)).astype(np.float32)
    f = np.linspace(1e-4, 15.0, 16, dtype=np.float32)
    z_all = np.concatenate([t_all[:, None], np.cos(f[None, :] * wpos[:, None]), -np.sin(f[None, :] * wpos[:, None])], axis=1).astype(np.float32)
    zT = np.ascontiguousarray(z_all[posc].transpose(2, 1, 0)).astype(np.float32)
    negt = (-t_all[posc] * valid).astype(np.float32)
    sgn = np.where(valid, 1.0, 0.0).astype(np.float32)
    mx = np.log(1e-2) / 0.3; mn = np.log(1e-2) / 1.5
    deltas = np.abs(np.linspace(mn, mx, D, dtype=np.float32)).astype(np.float32)
    mask = np.ones((128, 3), np.float32)
    return {"hymask": mask, "hyG": G, "hyG2": G2, "hyGT": GT, "hyS5": S5, "hyzT": zT.reshape(33, 2 * J * 128), "hynegt": negt, "hysgn": sgn, "hydelta": deltas}


def _bank_rr(L):
    st = {"i": 0}

    def nxt():
        b = L["psum"][st["i"] % 8]
        st["i"] += 1
        return b
    return nxt


KC = 4


def load_tab_bf16(k, dram_ap, dst, nm):
    shp = dst.ap.shape
    n_outer = shp[1]
    inner = int(np.prod(shp[2:]))
    st = k.sbt(nm + "_st", [128, inner], F32)
    for i_ in range(n_outer):
        src = dram_ap[:, i_]
        dsta = dst.ap[:, i_]
        if len(shp) == 4:
            src = src.rearrange("p a b -> p (a b)")
            dsta = dsta.rearrange("p a b -> p (a b)")
        k.dma("sp", st.ap[:], src, w=[st])
        k.op("dve", lambda e: e.tensor_copy(dsta, st.ap[:]), r=[st], w=[dst])


def emit_hy_filter(k, nc, L, Hd):
    from contextlib import ExitStack
    P = L["P"]
    J = L["J"]
    J4 = 4 * J
    dr = L["dram"]
    ps = L["psum"]
    stO = ExitStack()
    k.stack = stO
    rn = k.sbt("frn", [128, 2 * D], F32)
    stA = ExitStack()
    k.stack = stA
    Gs = k.sbt("fG", [128, J, 2, 128], BF16)
    load_tab_bf16(k, dr["hyG"], Gs, "fg1")
    ps_mlp = ps[0]; ps_hf = ps[1:3]; ps_nrm = ps[3:7]; ps_f1 = ps[7]
    w1 = k.sbt("fw1", [HY_EMB, 64], F32); w2 = k.sbt("fw2", [64, 64], F32); w3 = k.sbt("fw3", [64, 64], F32)
    w4 = k.sbt("fw4", [64, 4 * D], F32)
    fb = k.sbt("ffb", [64, 3, 2], F32)
    k.dma("sp", w1.ap[:], dr["hy_f_w1"], w=[w1]); k.dma("sp", w2.ap[:], dr["hy_f_w2"], w=[w2]); k.dma("sp", w3.ap[:], dr["hy_f_w3"], w=[w3])
    k.dma("sp", w4.ap[:], dr["hy_f_w4"], w=[w4])
    k.dma("sp", fb.ap[:, :, 0], dr["hy_f_freq"].rearrange("l f -> f l"), w=[fb], allow_slow_non_contiguous=True)
    for l_, nm in enumerate(["hy_f_b1", "hy_f_b2", "hy_f_b3"]):
        k.dma("sp", fb.ap[:, l_, 1:2], dr[nm].rearrange("(f o) -> f o", o=1), w=[fb], allow_slow_non_contiguous=True)
    k.op("dve", lambda e: e.tensor_tensor(fb.ap[:, :, 1], fb.ap[:, :, 1], fb.ap[:, :, 0], ALU.mult), w=[fb])
    zTs = [k.sbt("fzT%d" % i_, [HY_EMB, 512], F32) for i_ in range(2)]
    negt = k.sbt("fnegt", [128, 2 * J], F32); sgn = k.sbt("fsgn", [128, 2 * J], F32)
    k.dma("sp", negt.ap[:], dr["hynegt"], w=[negt]); k.dma("sp", sgn.ap[:], dr["hysgn"], w=[sgn])
    dbc = k.sbt("fdbc", [128, D], F32)
    k.dma("sp", dbc.ap[:], dr["hydelta"].partition_broadcast(128), w=[dbc])
    onesb = k.sbt("fones", [128, 128], BF16)
    k.op("dve", lambda e: e.memset(onesb.ap[:], 1.0), w=[onesb])
    G2 = k.sbt("fG2", [128, J, 2, 128], BF16)
    load_tab_bf16(k, dr["hyG2"], G2, "fg2")
    G = Gs
    arg = k.sbt("farg", [64, 512], F32); s2 = k.sbt("fs2", [64, 512], F32); s4 = k.sbt("fs4", [64, 512], F32)
    act_ = [k.sbt("fa%d" % i_, [64, 512], F32) for i_ in range(2)]
    dtmp = k.sbt("fdtmp", [128, D], F32); dec = k.sbt("fdec", [128, D], F32)
    taps = [[k.sbt("ftap%d_%d" % (b_, h_), [128, 2 * D], BF16) for h_ in range(2)] for b_ in range(1)]
    taps = [taps[0], taps[0]]
    absb = k.sbt("fabs", [128, 2 * D], BF16)
    afs = [k.sbt("fafs0", [128, 2, 2 * D], BF16)] * 2
    AFd = nc.dram_tensor("hyAFd" + L.get("sfx", ""), [2, 128, J, 2 * D], BF16, kind="Internal").ap()
    AFres = [k.dram_res("AFd%d" % j) for j in range(J)]
    for ch4 in range(max(1, (2 * J * 128) // 512)):
        zT = zTs[ch4 % 2]
        k.dma("sp", zT.ap[:], dr["hyzT"][:, ch4 * 512:(ch4 + 1) * 512], w=[zT])
        cur = zT.ap[:]
        srcs = [zT]
        for l_, w_ in enumerate([w1, w2, w3]):
            k.op("pe", lambda e: e.matmul(ps_mlp.ap[0:64, :], w_.ap[:], cur, start=True, stop=True), r=[w_] + srcs, w=[ps_mlp])
            k.op("dve", lambda e: e.tensor_scalar(arg.ap[:], ps_mlp.ap[0:64, :], fb.ap[:, l_, 0:1], fb.ap[:, l_, 1:2], ALU.mult, ALU.add), r=[ps_mlp, fb], w=[arg])
            k.op("act", lambda e: e.activation(s2.ap[:], arg.ap[:], AF.Sin, scale=0.5), r=[arg], w=[s2])
            k.op("act", lambda e: e.activation(s4.ap[:], arg.ap[:], AF.Sin, scale=0.25), r=[arg], w=[s4])
            k.op("dve", lambda e: e.tensor_tensor(s4.ap[:], s4.ap[:], s4.ap[:], ALU.mult), w=[s4])
            k.op("dve", lambda e: e.tensor_scalar(s4.ap[:], s4.ap[:], -2.0, 1.0, ALU.mult, ALU.add), w=[s4])
            a_ = act_[l_ % 2]
            k.op("dve", lambda e: e.scalar_tensor_tensor(a_.ap[:], s2.ap[:], 2.0, s4.ap[:], ALU.mult, ALU.mult), r=[s2, s4], w=[a_])
            cur = a_.ap[:]
            srcs = [a_]
        a3 = act_[0]
        for qi in range(4):
            q = ch4 * 4 + qi
            j, half = q // 2, q % 2
            tp = taps[j % 2][half]
            k.op("dve", lambda e: e.tensor_scalar(dtmp.ap[:], dbc.ap[:], negt.ap[:, q:q + 1], None, ALU.mult), r=[dbc, negt], w=[dtmp])
            k.op("act", lambda e: e.activation(dec.ap[:], dtmp.ap[:], AF.Exp), r=[dtmp], w=[dec])
            for o in range(2):
                for ch in range(2):
                    col = o * 2 * D + half * D + ch * 512
                    k.op("pe", lambda e: e.matmul(ps_hf[ch].ap[:], a3.ap[:, qi * 128:(qi + 1) * 128], w4.ap[:, col:col + 512], start=True, stop=True),
                         r=[a3, w4], w=[ps_hf[ch]])
                    k.op("dve", lambda e: e.scalar_tensor_tensor(tp.ap[:, o * D + ch * 512:o * D + (ch + 1) * 512], ps_hf[ch].ap[:], sgn.ap[:, q:q + 1],
                                                                 dec.ap[:, ch * 512:(ch + 1) * 512], ALU.mult, ALU.mult), r=[ps_hf[ch], sgn, dec], w=[tp])
            k.op("dve", lambda e: e.scalar_tensor_tensor(absb.ap[:], tp.ap[:], -1.0, tp.ap[:], ALU.mult, ALU.max), r=[tp], w=[absb])
            for cc in range(4):
                k.op("pe", lambda e: e.matmul(ps_nrm[cc].ap[:], onesb.ap[:], absb.ap[:, cc * 512:(cc + 1) * 512], start=(q == 0), stop=(q == 2 * J - 1)),
                     r=[onesb, absb], w=[ps_nrm[cc]])
            if half == 1:
                af = afs[j % 2]
                for ri in range(2):
                    for cc in range(4):
                        k.op("pe", lambda e: e.matmul(ps_f1.ap[:], G.ap[:, j, ri, :], taps[j % 2][0].ap[:, cc * 512:(cc + 1) * 512], start=True, stop=False),
                             r=[G, taps[j % 2][0]], w=[ps_f1])
                        k.op("pe", lambda e: e.matmul(ps_f1.ap[:], G2.ap[:, j, ri, :], taps[j % 2][1].ap[:, cc * 512:(cc + 1) * 512], start=False, stop=True),
                             r=[G2, taps[j % 2][1]], w=[ps_f1])
                        k.op("dve", lambda e: e.tensor_copy(af.ap[:, ri, cc * 512:(cc + 1) * 512], ps_f1.ap[:]), r=[ps_f1], w=[af])
                    k.dma("sp", AFd[ri, :, j, :], af.ap[:, ri, :], r=[af], w=[AFres[j]], sem_res=af)
    for cc in range(4):
        k.op("dve", lambda e: e.reciprocal(rn.ap[:, cc * 512:(cc + 1) * 512], ps_nrm[cc].ap[:]), r=[ps_nrm[cc]], w=[rn])
    kb_barrier(k)
    stA.close()
    stB = ExitStack()
    k.stack = stB
    S5 = k.sbt("fS5", [128, 5, 128], BF16)
    load_tab_bf16(k, dr["hyS5"], S5, "fs5")
    Bt = [k.sbt("fB%d" % i_, [128, KC, D], BF16) for i_ in range(2)]
    hst = [[k.sbt("fhs%d_%d" % (i_, ab), [128, KC, D], BF16) for ab in range(2)] for i_ in range(2)]
    it = 0
    for o in range(2):
        for q8 in range(64 // KC):
            B = Bt[it % 2]
            for ri in range(2):
                for hh in range(2):
                    pb = ri * 2 * J + hh * J
                    k.dma("sp", B.ap[pb:pb + J, :, :], AFd[ri, hh * 64 + q8 * KC:hh * 64 + q8 * KC + KC, :, o * D:(o + 1) * D].rearrange("k j c -> j k c"),
                          r=AFres, w=[B])
            for kl in range(KC):
                for ch in range(2):
                    for ab in range(2):
                        pb_ = ps[(ab * 2 + ch) % 8]
                        k.op("pe", lambda e: e.matmul(pb_.ap[0:J4, :], S5.ap[0:J4, 2 + ab, 0:J4], B.ap[0:J4, kl, ch * 512:(ch + 1) * 512], start=True, stop=True), r=[S5, B], w=[pb_])
                        k.op("dve", lambda e: e.tensor_tensor(hst[it % 2][ab].ap[0:J4, kl, ch * 512:(ch + 1) * 512], pb_.ap[0:J4, :], rn.ap[0:J4, o * D + ch * 512:o * D + (ch + 1) * 512], ALU.mult),
                             r=[pb_, rn], w=[hst[it % 2][ab]])
            for ab in range(2):
                k.dma("sp", Hd[o, ab, :, q8 * KC:(q8 + 1) * KC, :], hst[it % 2][ab].ap[0:J4], r=[hst[it % 2][ab]], w=[L["Hres"][o][ab][q8]], sem_res=hst[it % 2][ab])
            it += 1
    kb_barrier(k)
    stB.close()
    stO.close()


def emit_outproj_ln(k, L, O, xres, woutg, out_dram_ap, is_output=True):
    P = L["P"]
    T_ = L["T"]
    pt = L["ps_t"]
    for c in range(8):
        k.op("pe", lambda e: e.transpose(pt[c // 4].ap[:, (c % 4) * 128:(c % 4 + 1) * 128], O.ap[:, c * 128:(c + 1) * 128], P["identf"].ap[:]),
             r=[O, P["identf"]], w=[pt[c // 4]])
    GT = T_["GT"]
    for half in range(2):
        k.op("dve", lambda e: e.tensor_copy(GT.ap[:, half * 4:(half + 1) * 4, :], pt[half].ap[:].rearrange("p (a b) -> p a b", a=4)), r=[pt[half]], w=[GT])
    pp = L["ps_p"]
    for n in range(2):
        for kk in range(8):
            k.op("pe", lambda e: e.matmul(pp[n].ap[:], GT.ap[:, kk, :], woutg.ap[:, kk, n * 512:(n + 1) * 512], start=(kk == 0), stop=(kk == 7)),
                 r=[GT, woutg], w=[pp[n]])
    import os
    if os.environ.get("HYDBG") == "noln":
        for n in range(2):
            k.op("dve", lambda e: e.tensor_copy(xres.ap[:, n * 512:(n + 1) * 512], pp[n].ap[:]), r=[pp[n]], w=[xres])
        k.dma("sp", out_dram_ap, xres.ap[:], r=[xres], sem_res=xres, is_output=True)
        return
    emit_resid_ln(k, L, xres, pp, out_dram_ap, None, is_output)


def emit_hy_proj(k, nc, L, x, SC, mj=0):
    P = L["P"]
    dr = L["dram"]
    ps = L["psum"]
    pt = ps[0:2]
    J = L["J"]
    xv = x.rearrange("(p j) d -> j p d", j=J)
    win = P["win"]
    cw = k.sbt("hcw", [128, 3, 3 * D], F32)
    cb = k.sbt("hcb", [128, 3 * D], F32)
    for t in range(3):
        k.dma("sp", cw.ap[:, t, :], dr["hy_conv_w"][t, :].partition_broadcast(128), w=[cw])
    k.dma("sp", cb.ap[:], dr["hy_conv_b"].partition_broadcast(128), w=[cb])
    msk = k.sbt("hmask", [128, 3], F32)
    k.dma("sp", msk.ap[:], dr["hymask"], w=[msk])
    hTs = [k.sbt("hhT%d" % i_, [128, 8, 130], BF16) for i_ in range(2)]
    for h_ in hTs:
        k.op("pool", lambda e: e.memset(h_.ap[:], 0.0), w=[h_])
    xa = k.sbt("hxa", [128, D], F32)
    pre = [k.sbt("hpre%d" % i_, [128, 3 * D], F32) for i_ in range(3)]
    tmp = k.sbt("htmp", [128, 3 * D], F32)
    gt_ = k.sbt("hgt", [128, D], F32)
    sg = k.sbt("hsg", [128, D], F32)
    seq = [(J - 1, 0, -1)] + [(j, 1, j) for j in range(J)] + [(0, 2, J)]
    slot_of = {}
    k.dma("sp", xa.ap[:], xv[seq[0][0]], w=[xa])
    for it, (src, off, lj) in enumerate(seq):
        hT = hTs[it % 2]
        for c in range(8):
            k.op("pe", lambda e: e.transpose(pt[c // 4].ap[:, (c % 4) * 128:(c % 4 + 1) * 128], xa.ap[:, c * 128:(c + 1) * 128], P["identf"].ap[:]),
                 r=[xa, P["identf"]], w=[pt[c // 4]])
        if it + 1 < len(seq):
            k.dma("sp", xa.ap[:], xv[seq[it + 1][0]], w=[xa])
        for c in range(8):
            k.op("dve", lambda e: e.tensor_scalar(hT.ap[:, c, 1:129], pt[c // 4].ap[:, (c % 4) * 128:(c % 4 + 1) * 128],
                                                  P["modF"].ap[:, 8 + c, mj:mj + 1], P["modF"].ap[:, c, mj:mj + 1], ALU.mult, ALU.add),
                 r=[pt[c // 4], P["modF"]], w=[hT])
        pr = pre[it % 3]
        slot_of[lj] = pr
        ncol = 8 if off == 1 else 6
        for n in range(ncol):
            pb = ps[2 + n % 4]
            for kk in range(8):
                k.op("pe", lambda e: e.matmul(pb.ap[:], hT.ap[:, kk, off:off + 128], win.ap[:, kk, n * 512:(n + 1) * 512], start=(kk == 0), stop=(kk == 7)),
                     r=[hT, win], w=[pb])
            if n < 6:
                k.op("dve", lambda e: e.tensor_copy(pr.ap[:, n * 512:(n + 1) * 512], pb.ap[:]), r=[pb], w=[pr])
            else:
                k.op("dve", lambda e: e.tensor_copy(gt_.ap[:, (n - 6) * 512:(n - 5) * 512], pb.ap[:]), r=[pb], w=[gt_])
        if off == 1:
            k.op("act", lambda e: e.activation(sg.ap[:], gt_.ap[:], AF.Silu), r=[gt_], w=[sg])
            k.dma("sp", SC["sg"][lj], sg.ap[:], r=[sg], w=[SC["sg_res"][lj]], sem_res=sg)
        j = lj - 1
        if 0 <= j <= J - 1:
            a, b, c_ = slot_of[j - 1], slot_of[j], slot_of[j + 1]
            k.op("dve", lambda e: e.tensor_tensor(a.ap[:], a.ap[:], cw.ap[:, 0, :], ALU.mult), r=[cw], w=[a])
            k.op("pool", lambda e: e.tensor_tensor(tmp.ap[:], b.ap[:], cw.ap[:, 1, :], ALU.mult), r=[b, cw], w=[tmp])
            k.op("dve", lambda e: e.tensor_tensor(a.ap[:], a.ap[:], tmp.ap[:], ALU.add), r=[tmp], w=[a])
            k.op("pool", lambda e: e.tensor_tensor(tmp.ap[:], c_.ap[:], cw.ap[:, 2, :], ALU.mult), r=[c_, cw], w=[tmp])
            k.op("dve", lambda e: e.tensor_tensor(a.ap[:], a.ap[:], tmp.ap[:], ALU.add), r=[tmp], w=[a])
            k.op("pool", lambda e: e.tensor_tensor(a.ap[:], a.ap[:], cb.ap[:], ALU.add), r=[cb], w=[a])
            for gi, nm in enumerate(["v", "x1", "x2"]):
                k.dma("sp", SC[nm][j], a.ap[:, gi * D:(gi + 1) * D], r=[a], w=[SC[nm + "_res"][j]], sem_res=a)


def emit_hy_conv(k, nc, L, SC, o, Hd, epilogue, ep_alloc):
    from contextlib import ExitStack
    dr = L["dram"]
    J = L["J"]
    J4 = 4 * J
    ps = L["psum"]
    Ad = SC["Ad"]; Zd = SC["Zd"]
    NQ_ = 64 // KC
    with ExitStack() as st1:
        k.stack = st1
        G = k.sbt("cG", [128, J, 2, 128], BF16)
        load_tab_bf16(k, dr["hyG"], G, "cg")
        uf = k.sbt("cuf", [128, D], F32)
        ub = [k.sbt("cub%d" % i_, [128, D], BF16) for i_ in range(2)]
        As = [k.sbt("cAs%d" % i_, [128, 2, D], BF16) for i_ in range(2)]
        for j in range(J):
            k.dma("sp", uf.ap[:], SC["v"][j], r=[SC["v_res"][j]], w=[uf])
            u_ = ub[j % 2]
            k.op("pool", lambda e: e.tensor_copy(u_.ap[:], uf.ap[:]), r=[uf], w=[u_])
            a_ = As[j % 2]
            for ri in range(2):
                for ch in range(2):
                    pb = ps[(ri * 2 + ch) % 8]
                    k.op("pe", lambda e: e.matmul(pb.ap[:], G.ap[:, j, ri, :], u_.ap[:, ch * 512:(ch + 1) * 512], start=True, stop=True), r=[G, u_], w=[pb])
                    k.op("dve", lambda e: e.tensor_copy(a_.ap[:, ri, ch * 512:(ch + 1) * 512], pb.ap[:]), r=[pb], w=[a_])
                k.dma("sp", Ad[ri, :, j, :], a_.ap[:, ri, :], r=[a_], w=[SC["Ad_res"][j]], sem_res=a_)
        kb_barrier(k)
    with ExitStack() as st2:
        k.stack = st2
        S5 = k.sbt("cS5", [128, 5, 128], BF16)
        load_tab_bf16(k, dr["hyS5"], S5, "cs5")
        Bt = [k.sbt("cB%d" % i_, [128, KC, D], BF16) for i_ in range(2)]
        Ht = [[k.sbt("cH%d_%d" % (i_, ab), [128, KC, D], BF16) for ab in range(2)] for i_ in range(2)]
        Yt = k.sbt("cY", [128, KC, D], BF16)
        yts = [[k.sbt("cyt%d_%d" % (a_, b_), [128, 512], F32) for b_ in range(2)] for a_ in range(2)]
        Zs = [k.sbt("cZs%d" % i_, [128, KC, D], BF16) for i_ in range(2)]
        for q in range(NQ_):
            B = Bt[q % 2]
            for ri in range(2):
                for hh in range(2):
                    pb0 = ri * 2 * J + hh * J
                    k0 = hh * 64 + q * KC
                    k.dma("sp", B.ap[pb0:pb0 + J, :, :], Ad[ri, k0:k0 + KC, :, :].rearrange("k j c -> j k c"), r=SC["Ad_res"], w=[B])
            for ab in range(2):
                k.dma("sp", Ht[q % 2][ab].ap[0:J4], Hd[o, ab, :, q * KC:(q + 1) * KC, :], r=[L["Hres"][o][ab][q]], w=[Ht[q % 2][ab]])
            zs = Zs[q % 2]
            Htq = Ht[q % 2]
            its = [(kl, ch) for kl in range(KC) for ch in range(2)]

            def stage_s(i_):
                kl, ch = its[i_]
                sl = slice(ch * 512, (ch + 1) * 512)
                pa = ps[0 + i_ % 2]; pbk = ps[2 + i_ % 2]
                k.op("pe", lambda e: e.matmul(pa.ap[0:J4, :], S5.ap[0:J4, 0, 0:J4], B.ap[0:J4, kl, sl], start=True, stop=True), r=[S5, B], w=[pa])
                k.op("pe", lambda e: e.matmul(pbk.ap[0:J4, :], S5.ap[0:J4, 1, 0:J4], B.ap[0:J4, kl, sl], start=True, stop=True), r=[S5, B], w=[pbk])

            def stage_y(i_):
                kl, ch = its[i_]
                sl = slice(ch * 512, (ch + 1) * 512)
                pa = ps[0 + i_ % 2]; pbk = ps[2 + i_ % 2]
                y1 = yts[i_ % 2][0]; y2 = yts[i_ % 2][1]
                k.op("dve", lambda e: e.tensor_tensor(y1.ap[0:J4, :], pa.ap[0:J4, :], Htq[0].ap[0:J4, kl, sl], ALU.mult), r=[pa, Htq[0]], w=[y1])
                k.op("dve", lambda e: e.tensor_tensor(y2.ap[0:J4, :], pbk.ap[0:J4, :], Htq[1].ap[0:J4, kl, sl], ALU.mult), r=[pbk, Htq[1]], w=[y2])
                k.op("pool", lambda e: e.tensor_tensor(Yt.ap[0:J4, kl, sl], y1.ap[0:J4, :], y2.ap[0:J4, :], ALU.add), r=[y1, y2], w=[Yt])

            def stage_z(i_):
                kl, ch = its[i_]
                sl = slice(ch * 512, (ch + 1) * 512)
                pz = ps[4 + i_ % 2]
                k.op("pe", lambda e: e.matmul(pz.ap[0:J4, :], S5.ap[0:J4, 4, 0:J4], Yt.ap[0:J4, kl, sl], start=True, stop=True), r=[S5, Yt], w=[pz])
                k.op("dve", lambda e: e.tensor_copy(zs.ap[0:J4, kl, sl], pz.ap[0:J4, :]), r=[pz], w=[zs])

            stage_s(0)
            for i_ in range(len(its)):
                stage_y(i_)
                if i_ + 1 < len(its):
                    stage_s(i_ + 1)
                if i_ >= 1:
                    stage_z(i_ - 1)
            stage_z(len(its) - 1)
            for ri in range(2):
                for hh in range(2):
                    pb0 = ri * 2 * J + hh * J
                    k0 = hh * 64 + q * KC
                    k.dma("sp", Zd[ri, k0:k0 + KC, :, :].rearrange("k j c -> j k c"), zs.ap[pb0:pb0 + J, :, :], r=[zs], w=[SC["Zd_res"][q]], sem_res=zs)
        kb_barrier(k)
    with ExitStack() as st3:
        k.stack = st3
        GTt = k.sbt("cGT", [128, J, 2, 128], BF16)
        load_tab_bf16(k, dr["hyGT"], GTt, "cgt")
        Zin = [k.sbt("cZin%d" % i_, [128, 2, D], BF16) for i_ in range(2)]
        E = ep_alloc()
        for j in range(J):
            zi = Zin[j % 2]
            for ri in range(2):
                k.dma("sp", zi.ap[:, ri, :], Zd[ri, :, j, :], r=SC["Zd_res"], w=[zi])
            ybanks = [ps[4 + 2 * (j % 2)], ps[5 + 2 * (j % 2)]]
            for ch in range(2):
                for ri in range(2):
                    k.op("pe", lambda e: e.matmul(ybanks[ch].ap[:], GTt.ap[:, j, ri, :], zi.ap[:, ri, ch * 512:(ch + 1) * 512], start=(ri == 0), stop=(ri == 1)),
                         r=[GTt, zi], w=[ybanks[ch]])
            epilogue(j, ybanks, E)
        kb_barrier(k)


def build_hy_layer(ctx_mode, stop_after=None, env=None):
    from contextlib import ExitStack
    J = 2 if ctx_mode else 32
    SEQ = 128 * J
    if env is None:
        nc = bass.Bass("TRN2", target_bir_lowering=False)
        dt_in = lambda n, s_: nc.dram_tensor(n, list(s_), F32, kind="ExternalInput").ap()
    else:
        nc = env["nc"]
        dt_in = lambda n, s_: env["aps"][n]
    L = {"J": J}
    x = dt_in("x", [SEQ, D])
    L["c"] = dt_in("c", [D]); L["c_ctx"] = dt_in("c_ctx", [D])
    L["w_ada"] = dt_in("w_ada", [D, 3 * D]); L["b_ada"] = dt_in("b_ada", [3 * D])
    L["w_in"] = dt_in("w_in", [D, 4 * D]); L["w_out"] = dt_in("w_out", [D, D])
    L["ln_g"] = dt_in("ln_g", [D]); L["ln_b"] = dt_in("ln_b", [D])
    idd = dt_in("identf_d", [128, 128])
    dr = {}
    L["dram"] = dr
    for nm, shp in [("hy_conv_w", [3, 3 * D]), ("hy_conv_b", [3 * D]), ("hy_f_w1", [HY_EMB, 64]), ("hy_f_b1", [64]), ("hy_f_w2", [64, 64]),
                    ("hy_f_b2", [64]), ("hy_f_w3", [64, 64]), ("hy_f_b3", [64]), ("hy_f_w4", [64, 4 * D]), ("hy_f_freq", [3, 64]), ("hy_skip", [2, D]),
                    ("hyG", [128, J, 2, 128]), ("hyG2", [128, J, 2, 128]), ("hyGT", [128, J, 2, 128]), ("hyS5", [128, 5, 128]),
                    ("hymask", [128, 3]), ("hyzT", [HY_EMB, 2 * J * 128]), ("hynegt", [128, 2 * J]), ("hysgn", [128, 2 * J]), ("hydelta", [D])]:
        dr[nm] = dt_in(nm, shp)
    if env is None:
        x_out = nc.dram_tensor("x_out", [SEQ, D], F32, kind="ExternalOutput").ap()
        k = KB(nc)
    else:
        x_out = env["aps"]["x_out"]
        k = env["k"]
    P = {}
    L["P"] = P
    P["identf"] = k.sb("identf", [128, 128], F32)
    P["eps"] = k.sb("eps", [128, 1], F32)
    P["modF"] = k.sb("modF", [128, 24, 2], F32)
    P["g_bc"] = k.sb("g_bc", [128, D], F32)
    P["b_bc"] = k.sb("b_bc", [128, D], F32)
    P["woutg"] = k.sb("woutg", [128, 8, D], BF16)
    P["woutgc"] = P["woutg"]
    P["win"] = k.sb("win", [128, 8, 4 * D], BF16)
    psum = env["psum"] if env is not None else [k.ps("psb%d" % i_, [128, 512], F32) for i_ in range(8)]
    L["psum"] = psum
    L["ps_t"] = psum[0:2]; L["ps_p"] = psum[2:4]
    T_ = {}
    L["T"] = T_
    T_["GT"] = k.sb("GT", [128, 8, 128], BF16)
    T_["st"] = k.sb("st", [128, 2, 6], F32)
    T_["mv"] = k.sb("mv", [128, 2], F32)
    T_["dummy"] = k.sb("dummyact", [128, 1], F32)
    k.dma("sp", P["identf"].ap[:], idd, w=[P["identf"]])
    k.op("dve", lambda e: e.memset(P["eps"].ap[:], LN_EPS), w=[P["eps"]])
    sfx = env["sfx"] if env is not None else ""
    Hd = nc.dram_tensor("hyHd" + sfx, [2, 2, 4 * J, 64, D], BF16, kind="Internal").ap()
    L["Hres"] = [[[k.dram_res("H%d%d%d" % (o, ab, q)) for q in range(64 // KC)] for ab in range(2)] for o in range(2)]
    SC = {}
    for nm in ["v", "x1", "x2", "sg"]:
        t_ = nc.dram_tensor("hy_" + nm + sfx, [J, 128, D], F32, kind="Internal").ap()
        SC[nm] = [t_[j] for j in range(J)]
        SC[nm + "_res"] = [k.dram_res(nm + "%d" % j) for j in range(J)]
    SC["Ad"] = nc.dram_tensor("hyAd" + sfx, [2, 128, J, D], BF16, kind="Internal").ap()
    SC["Zd"] = nc.dram_tensor("hyZd" + sfx, [2, 128, J, D], BF16, kind="Internal").ap()
    SC["Ad_res"] = [k.dram_res("Ad%d" % j) for j in range(J)]
    SC["Zd_res"] = [k.dram_res("Zd%d" % q) for q in range(64 // KC)]
    xv = x.rearrange("(p j) d -> j p d", j=J)
    xov = x_out.rearrange("(p j) d -> j p d", j=J)
    L["sfx"] = sfx
    emit_hy_filter(k, nc, L, Hd)
    if stop_after == "filter":
        L["Hd"] = Hd
        k.finish(); return nc, k, L
    with ExitStack() as stp:
        k.stack = stp
        emit_prep(k, nc, L, 0, False, gate_sel=1 if ctx_mode else 0, nbuf=2)
        kb_barrier(k)
    with ExitStack() as stq:
        k.stack = stq
        emit_hy_proj(k, nc, L, x, SC, mj=1 if ctx_mode else 0)
        kb_barrier(k)
    if stop_after == "proj":
        k.finish(); return nc, k, L
    skb = k.sb("skipbc", [128, 2, D], F32)
    for o in range(2):
        k.dma("sp", skb.ap[:, o, :], dr["hy_skip"][o, :].partition_broadcast(128), w=[skb])

    def ep_alloc1():
        return {"vt": k.sbt("e_vt", [128, D], F32), "x1t": k.sbt("e_x1", [128, D], F32)}

    def epi1(j, yb, E):
        vt, x1t = E["vt"], E["x1t"]
        k.dma("sp", vt.ap[:], SC["v"][j], r=[SC["v_res"][j]], w=[vt])
        k.dma("sp", x1t.ap[:], SC["x1"][j], r=[SC["x1_res"][j]], w=[x1t])
        k.op("pool", lambda e: e.tensor_tensor(vt.ap[:], vt.ap[:], skb.ap[:, 0, :], ALU.mult), r=[skb], w=[vt])
        for ch in range(2):
            k.op("dve", lambda e: e.tensor_tensor(vt.ap[:, ch * 512:(ch + 1) * 512], vt.ap[:, ch * 512:(ch + 1) * 512], yb[ch].ap[:], ALU.add), r=[yb[ch]], w=[vt])
        k.op("pool", lambda e: e.tensor_tensor(vt.ap[:], vt.ap[:], x1t.ap[:], ALU.mult), r=[x1t], w=[vt])
        k.dma("sp", SC["v"][j], vt.ap[:], r=[vt], w=[SC["v_res"][j]], sem_res=vt)

    emit_hy_conv(k, nc, L, SC, 0, Hd, epi1, ep_alloc1)
    if stop_after == "conv1":
        k.finish(); return nc, k, L

    def ep_alloc2():
        return {"vt": k.sbt("e_vt", [128, D], F32), "x2t": k.sbt("e_x2", [128, D], F32), "sgt": k.sbt("e_sg", [128, D], F32),
                "xr": [k.sbt("e_xr%d" % i_, [128, D], F32) for i_ in range(2)]}

    def epi2(j, yb, E):
        vt, x2t, sgt = E["vt"], E["x2t"], E["sgt"]
        xr = E["xr"][j % 2]
        k.dma("sp", vt.ap[:], SC["v"][j], r=[SC["v_res"][j]], w=[vt])
        k.dma("sp", x2t.ap[:], SC["x2"][j], r=[SC["x2_res"][j]], w=[x2t])
        k.dma("sp", sgt.ap[:], SC["sg"][j], r=[SC["sg_res"][j]], w=[sgt])
        k.dma("sp", xr.ap[:], xv[j], w=[xr])
        k.op("pool", lambda e: e.tensor_tensor(vt.ap[:], vt.ap[:], skb.ap[:, 1, :], ALU.mult), r=[skb], w=[vt])
        for ch in range(2):
            k.op("dve", lambda e: e.tensor_tensor(vt.ap[:, ch * 512:(ch + 1) * 512], vt.ap[:, ch * 512:(ch + 1) * 512], yb[ch].ap[:], ALU.add), r=[yb[ch]], w=[vt])
        k.op("pool", lambda e: e.tensor_tensor(vt.ap[:], vt.ap[:], x2t.ap[:], ALU.mult), r=[x2t], w=[vt])
        k.op("pool", lambda e: e.tensor_tensor(vt.ap[:], vt.ap[:], sgt.ap[:], ALU.mult), r=[sgt], w=[vt])
        import os
        if os.environ.get("HYDBG") == "noproj":
            k.dma("sp", xov[j], vt.ap[:], r=[vt], sem_res=vt, is_output=True)
        else:
            emit_outproj_ln(k, L, vt, xr, P["woutg"], xov[j], True)

    emit_hy_conv(k, nc, L, SC, 1, Hd, epi2, ep_alloc2)
    if env is not None:
        return nc, k, L
    k.finish()
    return nc, k, L


_PROG = {}

_HY_W = ["hy_conv_w", "hy_conv_b", "hy_f_w1", "hy_f_b1", "hy_f_w2", "hy_f_b2", "hy_f_w3", "hy_f_b3", "hy_f_w4", "hy_f_freq", "hy_skip"]
_HY_W_SHAPES = {"hy_conv_w": [3, 3 * D], "hy_conv_b": [3 * D], "hy_f_w1": [HY_EMB, 64], "hy_f_b1": [64], "hy_f_w2": [64, 64], "hy_f_b2": [64],
                "hy_f_w3": [64, 64], "hy_f_b3": [64], "hy_f_w4": [64, 4 * D], "hy_f_freq": [3, 64], "hy_skip": [2, D]}
_HY_T_SHARED = {"hydelta": [D], "hyS5": [128, 5, 128], "hymask": [128, 3]}


def _hy_t_mode(J):
    return {"hyG": [128, J, 2, 128], "hyG2": [128, J, 2, 128], "hyGT": [128, J, 2, 128],
            "hyzT": [HY_EMB, 2 * J * 128], "hynegt": [128, 2 * J], "hysgn": [128, 2 * J]}


def build_fused():
    from contextlib import ExitStack
    nc = bass.Bass("TRN2", target_bir_lowering=False)
    din = lambda n, s_: nc.dram_tensor(n, list(s_), F32, kind="ExternalInput").ap()
    A = {}
    A["x"] = din("x", [S, D]); A["ctx"] = din("ctx", [CTX, D]); A["c"] = din("c", [D]); A["c_ctx"] = din("c_ctx", [D])
    A["w_ada"] = din("w_ada", [4, D, 3 * D]); A["b_ada"] = din("b_ada", [4, 3 * D]); A["w_in"] = din("w_in", [4, D, 4 * D])
    A["w_out"] = din("w_out", [4, D, D]); A["ln_g"] = din("ln_g", [4, D]); A["ln_b"] = din("ln_b", [4, D])
    A["tb_d"] = din("tb_d", [2, 128, 8, NSLOT * 64]); A["identf_d"] = din("identf_d", [128, 128])
    for nm in _HY_W:
        A[nm] = din(nm, [2] + _HY_W_SHAPES[nm])
    for nm, shp in _HY_T_SHARED.items():
        A[nm] = din(nm, shp)
        if nm != "hydelta":
            A[nm + "_c"] = din(nm + "_c", shp)
    for nm, shp in _hy_t_mode(32).items():
        A[nm] = din(nm, shp)
    for nm, shp in _hy_t_mode(2).items():
        A[nm + "_c"] = din(nm + "_c", shp)
    y = nc.dram_tensor("y_out", [S, D], F32, kind="ExternalOutput").ap()
    xs0 = nc.dram_tensor("xs0", [S, D], F32, kind="Internal").ap()
    xs1 = nc.dram_tensor("xs1", [S, D], F32, kind="Internal").ap()
    cs0 = nc.dram_tensor("cs0", [CTX, D], F32, kind="Internal").ap()
    cs1 = nc.dram_tensor("cs1", [CTX, D], F32, kind="Internal").ap()
    k = KB(nc)
    psum = [k.ps("psb%d" % i_, [128, 512], F32) for i_ in range(8)]

    def run_layer(fn):
        k.layer_stack = ExitStack()
        fn()
        kb_barrier(k)
        k.layer_stack.close()
        k.layer_stack = None
        k.recycle()

    def common(i):
        return {"c": A["c"], "c_ctx": A["c_ctx"], "w_ada": A["w_ada"][i], "b_ada": A["b_ada"][i], "w_in": A["w_in"][i], "w_out": A["w_out"][i],
                "ln_g": A["ln_g"][i], "ln_b": A["ln_b"][i], "identf_d": A["identf_d"]}

    def hy_aps(j, ctx_mode):
        d_ = {nm: A[nm][j] for nm in _HY_W}
        for nm in list(_HY_T_SHARED) + list(_hy_t_mode(2)):
            d_[nm] = A[nm + "_c"] if (ctx_mode and nm != "hydelta") else A[nm]
        return d_

    run_layer(lambda: build_na_layer(True, env={"nc": nc, "k": k, "psum": psum,
                                                 "aps": dict(common(0), x=A["x"], ctx=A["ctx"], tb_d=A["tb_d"][0], x_out=xs0, ctx_out=cs0)}))
    run_layer(lambda: build_hy_layer(False, env={"nc": nc, "k": k, "psum": psum, "sfx": "_a",
                                                  "aps": dict(common(1), **hy_aps(0, False), x=xs0, x_out=xs1)}))
    run_layer(lambda: build_hy_layer(True, env={"nc": nc, "k": k, "psum": psum, "sfx": "_b",
                                                 "aps": dict(common(1), **hy_aps(0, True), x=cs0, x_out=cs1)}))
    run_layer(lambda: build_na_layer(False, env={"nc": nc, "k": k, "psum": psum,
                                                  "aps": dict(common(2), x=xs1, ctx=cs1, tb_d=A["tb_d"][1], x_out=xs0)}))
    run_layer(lambda: build_hy_layer(False, env={"nc": nc, "k": k, "psum": psum, "sfx": "_c",
                                                  "aps": dict(common(3), **hy_aps(1, False), x=xs0, x_out=y)}))
    kb_barrier(k)
    return nc, k


def kernel(**inputs):
    inp = {k_: np.ascontiguousarray(np.asarray(v, dtype=np.float32)) for k_, v in inputs.items()}
    B = inp["x"].shape[0]
    if "fused" not in _PROG:
        _PROG["fused"] = build_fused()[0]
    shared = {nm: inp[nm] for nm in ["c_ctx", "w_ada", "b_ada", "w_in", "w_out", "ln_g", "ln_b"] + _HY_W}
    shared["tb_d"] = np.stack([na_table_host(inp["na_rpb"][0]), na_table_host(inp["na_rpb"][1])], axis=0)
    shared["identf_d"] = np.eye(128, dtype=np.float32)
    tl = hy_tables(32)
    tc = hy_tables(2)
    for nm in list(_HY_T_SHARED) + list(_hy_t_mode(2)):
        shared[nm] = tl[nm]
        if nm != "hydelta":
            shared[nm + "_c"] = tc[nm]
    in_maps = [dict(shared, x=inp["x"][b], ctx=inp["ctx"][b], c=inp["c"][b]) for b in range(B)]
    res = run_bass_kernel_spmd(_PROG["fused"], in_maps, core_ids=list(range(B)))
    return np.stack([res.results[b]["y_out"] for b in range(B)], axis=0).astype(np.float32)
```

```python
import numpy as np
import concourse.bass as bass
import concourse.mybir as mybir
from concourse.bass_utils import run_bass_kernel_spmd

F32 = mybir.dt.float32
BF16 = mybir.dt.bfloat16
AF = mybir.ActivationFunctionType
ALU = mybir.AluOpType
AX = mybir.AxisListType


LOAD_Q = "act"


class Res:
    __slots__ = ("name", "ap", "last_w", "reads", "sem", "semcnt", "excl")

    def __init__(self, name, ap=None):
        self.name = name
        self.ap = ap
        self.last_w = {}
        self.reads = {}
        self.sem = None
        self.semcnt = 0
        self.excl = False


class KB:
    def __init__(self, nc):
        self.nc = nc
        self.eng = {"pe": nc.tensor, "act": nc.scalar, "dve": nc.vector, "pool": nc.gpsimd, "sp": nc.sync}
        self.esem = {}
        self.ecnt = {}
        self.waited = {}
        for k_ in self.eng:
            self.esem[k_] = nc.alloc_semaphore(name="es_" + k_)
            self.ecnt[k_] = 0
            self.waited[k_] = {}
        self.n_inst = 0
        self.out_events = {}
        self.all_dma = {}

    def _uniq(self, name):
        self.uid = getattr(self, "uid", 0) + 1
        return "%s_u%d" % (name, self.uid)

    def sb(self, name, shape, dtype):
        name = self._uniq(name)
        if getattr(self, "layer_stack", None) is not None:
            t = self.layer_stack.enter_context(self.nc.sbuf_tensor(name, list(shape), dtype))
            return Res(name, t.ap() if hasattr(t, "ap") and callable(t.ap) else t)
        t = self.nc.alloc_sbuf_tensor(name, list(shape), dtype)
        return Res(name, t.ap() if hasattr(t, "ap") and callable(t.ap) else t)

    def sbt(self, name, shape, dtype):
        name = self._uniq(name)
        t = self.stack.enter_context(self.nc.sbuf_tensor(name, list(shape), dtype))
        return Res(name, t.ap() if hasattr(t, "ap") and callable(t.ap) else t)

    def ps(self, name, shape, dtype):
        t = self.nc.alloc_psum_tensor(name, list(shape), dtype)
        rr = Res(name, t.ap() if hasattr(t, "ap") and callable(t.ap) else t)
        rr.excl = True
        return rr

    def dram_res(self, name):
        return Res(name)

    def _wait_deps(self, ek, reads, writes):
        deps = {}

        def add(d):
            for sid, (sem, val) in d.items():
                if sid not in deps or deps[sid][1] < val:
                    deps[sid] = (sem, val)

        for r in reads:
            add(r.last_w)
        for w_ in writes:
            add(w_.last_w)
            add(w_.reads)
        eng = self.eng[ek]
        wd = self.waited[ek]
        own = id(self.esem[ek])
        for sid, (sem, val) in deps.items():
            if ek == "pe" and sid == own:
                continue
            if wd.get(sid, 0) >= val:
                continue
            eng.wait_ge(sem, val)
            wd[sid] = val

    def _record(self, ev, reads, writes):
        sid = id(ev[0])
        for r in reads:
            cur = r.reads.get(sid)
            if cur is None or cur[1] < ev[1]:
                r.reads[sid] = ev
        for w_ in writes:
            w_.last_w = {sid: ev}
            w_.reads = {}

    def op(self, ek, fn, r=(), w=()):
        w = list(w) + [x for x in r if x.excl and x not in w]
        r = [x for x in r if not x.excl]
        self._wait_deps(ek, r, w)
        inst = fn(self.eng[ek])
        self.ecnt[ek] += 1
        inst.then_inc(self.esem[ek], 1)
        ev = (self.esem[ek], self.ecnt[ek])
        self._record(ev, r, w)
        self.n_inst += 1
        return inst

    def dma(self, ek, out_ap, in_ap, r=(), w=(), sem_res=None, is_output=False, **kw):
        if ek == "sp" and any(x.ap is not None for x in w):
            ek = LOAD_Q
        self._wait_deps(ek, r, w)
        if sem_res is None:
            cands = [x for x in list(w) + list(r) if x.ap is not None]
            sem_res = cands[0] if cands else (list(w) + list(r))[0]
        if sem_res.sem is None:
            pool = getattr(self, "sem_pool", None)
            if pool:
                sem_res.sem, sem_res.semcnt = pool.pop()
            else:
                self.nsem = getattr(self, "nsem", 0) + 1
                sem_res.sem = self.nc.alloc_semaphore(name="ds%d_%s" % (self.nsem, sem_res.name))
            if not hasattr(self, "sem_live"):
                self.sem_live = []
            self.sem_live.append(sem_res)
        inst = self.eng[ek].dma_start(out=out_ap, in_=in_ap, **kw)
        sem_res.semcnt += 16
        inst.then_inc(sem_res.sem, 16)
        ev = (sem_res.sem, sem_res.semcnt)
        self._record(ev, r, w)
        self.all_dma[id(sem_res.sem)] = ev
        if is_output:
            self.out_events[id(sem_res.sem)] = ev
        self.n_inst += 1
        return inst

    def recycle(self):
        if not hasattr(self, "sem_pool"):
            self.sem_pool = []
        for r_ in getattr(self, "sem_live", []):
            self.sem_pool.append((r_.sem, r_.semcnt))
            r_.sem = None
        self.sem_live = []

    def finish(self):
        for sid, (sem, val) in self.out_events.items():
            self.eng["sp"].wait_ge(sem, val)


D = 1024
S = 4096
NT = 32
H = 16
DH = 64
CTX = 256
GW = 64
KH = 8
KW = 16
ALPHA = (2 * 4) ** 0.25
LN_EPS = 1e-5
NSLOT = 22
NEG = -1e30


def _r0(r):
    return min(max(r - KH // 2, 0), GW - KH)


def na_bias_index():
    sent = H * 15 * 31
    idx = np.full((64, H, NSLOT, 64), sent, dtype=np.int64)
    slot_delta = {}
    for s in range(1, 16):
        slot_delta[s] = s - 8
    slot_delta[18] = -4
    slot_delta[19] = 3
    qc = np.arange(64)[:, None]
    kc = np.arange(64)[None, :]
    qs = np.clip(qc - KW // 2, 0, GW - KW)
    inwin = (kc >= qs) & (kc < qs + KW)
    dcol = np.clip(kc - qc + KW - 1, 0, 2 * KW - 2)
    for s, dl in slot_delta.items():
        drow = dl + 7
        for h in range(H):
            v = h * 15 * 31 + drow * 31 + dcol
            idx[:, h, s, :] = np.where(inwin, v, sent)
    return idx


def na_slot(r, kt):
    r0 = _r0(r)
    ka, kb = 2 * kt, 2 * kt + 1
    va = r0 <= ka <= r0 + KH - 1
    vb = r0 <= kb <= r0 + KH - 1
    if va and vb:
        return ka - r + 8
    if vb:
        assert kb - r == -4
        return 17
    if va:
        assert ka - r == 3
        return 19
    return 20


def na_key_tiles(T):
    rows = [2 * T, 2 * T + 1]
    lo = min(_r0(r) for r in rows) // 2
    hi = max(_r0(r) + KH - 1 for r in rows) // 2
    return list(range(lo, hi + 1))


def kb_barrier(k):
    evs = [(k.esem[e], k.ecnt[e]) for e in k.eng if k.ecnt[e] > 0]
    evs += list(k.all_dma.values())
    for e in k.eng:
        wd = k.waited[e]
        for sem, val in evs:
            if wd.get(id(sem), 0) < val:
                k.eng[e].wait_ge(sem, val)
                wd[id(sem)] = val


def emit_prep(k, nc, L, i, use_ctx_out, gate_sel=0, nbuf=1):
    P = L["P"]
    craw = k.sbt("craw", [128, 8, 2], F32)
    scT = k.sbt("scT", [128, 8, 2], F32)
    k.dma("sp", craw.ap[:, :, 0], L["c"].rearrange("(k p) -> p k", p=128), w=[craw], allow_slow_non_contiguous=True)
    k.dma("sp", craw.ap[:, :, 1], L["c_ctx"].rearrange("(k p) -> p k", p=128), w=[craw], allow_slow_non_contiguous=True)
    k.op("act", lambda e: e.activation(scT.ap[:].rearrange("p a b -> p (a b)"), craw.ap[:].rearrange("p a b -> p (a b)"), AF.Silu), r=[craw], w=[scT])
    ones = k.sbt("ones128", [128, 128], F32)
    k.op("dve", lambda e: e.memset(ones.ap[:], 1.0), w=[ones])
    scB = k.sbt("scB", [128, 2, 8, 128], F32)
    for j in range(2):
        for kk in range(8):
            k.op("dve", lambda e: e.tensor_scalar(scB.ap[:, j, kk, :], ones.ap[:], scT.ap[:, kk, j:j + 1], None, ALU.mult), r=[ones, scT], w=[scB])
    bF = k.sbt("bF", [128, 24, 2], F32)
    for j in range(2):
        k.dma("sp", bF.ap[:, :, j], L["b_ada"].rearrange("(m p) -> p m", p=128), w=[bF], allow_slow_non_contiguous=True)
    bG = k.sbt("bG", [128, 1024], F32)
    k.dma("sp", bG.ap[:], L["b_ada"][2048:3072].partition_broadcast(128), w=[bG])
    k.dma("sp", P["g_bc"].ap[:], L["ln_g"].partition_broadcast(128), w=[P["g_bc"]])
    k.dma("sp", P["b_bc"].ap[:], L["ln_b"].partition_broadcast(128), w=[P["b_bc"]])
    ps_mod = L["psum"][0]
    ps_g = L["psum"][1:5]
    was = [k.sbt("wa%d" % s_, [128, 3072], F32) for s_ in range(nbuf)]
    for kk in range(8):
        wa = was[kk % nbuf]
        k.dma("sp", wa.ap[:], L["w_ada"][kk * 128:(kk + 1) * 128, :], w=[wa])
        for m in range(24):
            k.op("pe", lambda e: e.matmul(ps_mod.ap[:, kk * 48 + 2 * m:kk * 48 + 2 * m + 2], wa.ap[:, m * 128:(m + 1) * 128], scT.ap[:, kk, :],
                                          start=True, stop=True), r=[wa, scT], w=[ps_mod])
        for j in range(2):
            for n in range(2):
                pg = ps_g[j * 2 + n]
                k.op("pe", lambda e: e.matmul(pg.ap[:], scB.ap[:, j, kk, :], wa.ap[:, 2048 + n * 512:2048 + (n + 1) * 512],
                                              start=(kk == 0), stop=(kk == 7)), r=[wa, scB], w=[pg])
    modF = P["modF"]
    k.op("dve", lambda e: e.tensor_tensor(modF.ap[:].rearrange("p a b -> p (a b)"), ps_mod.ap[:, 0:48], bF.ap[:].rearrange("p a b -> p (a b)"), ALU.add),
         r=[ps_mod, bF], w=[modF])
    for kk in range(1, 8):
        k.op("dve", lambda e: e.tensor_tensor(modF.ap[:].rearrange("p a b -> p (a b)"), ps_mod.ap[:, kk * 48:(kk + 1) * 48], modF.ap[:].rearrange("p a b -> p (a b)"), ALU.add),
             r=[ps_mod], w=[modF])
    k.op("dve", lambda e: e.tensor_scalar(modF.ap[:, 8:16, :], modF.ap[:, 8:16, :], 1.0, None, ALU.add), r=[modF], w=[modF])
    gates = [k.sbt("gate%d" % j, [128, 1024], F32) for j in range(2)]
    for j in range(2):
        for n in range(2):
            pg = ps_g[j * 2 + n]
            k.op("dve", lambda e: e.tensor_tensor(gates[j].ap[:, n * 512:(n + 1) * 512], pg.ap[:], bG.ap[:, n * 512:(n + 1) * 512], ALU.add),
                 r=[pg, bG], w=[gates[j]])
    wos = [k.sbt("wos%d" % s_, [128, 1024], F32) for s_ in range(nbuf)]
    for kk in range(8):
        wo = wos[kk % nbuf]
        k.dma("sp", wo.ap[:], L["w_out"][kk * 128:(kk + 1) * 128, :], w=[wo])
        k.op("dve", lambda e: e.tensor_tensor(P["woutg"].ap[:, kk, :], wo.ap[:], gates[gate_sel].ap[:], ALU.mult), r=[wo, gates[gate_sel]], w=[P["woutg"]])
        if use_ctx_out:
            k.op("pool", lambda e: e.tensor_tensor(P["woutgc"].ap[:, kk, :], wo.ap[:], gates[1].ap[:], ALU.mult), r=[wo, gates[1]], w=[P["woutgc"]])
    wis = [k.sbt("wis%d" % s_, [128, 1024], F32) for s_ in range(max(1, 2 * nbuf - 1))]
    q = 0
    for kk in range(8):
        for n in range(4):
            wi = wis[q % max(1, 2 * nbuf - 1)]
            k.dma("sp", wi.ap[:], L["w_in"][kk * 128:(kk + 1) * 128, n * 1024:(n + 1) * 1024], w=[wi])
            if q % 2 == 0:
                k.op("dve", lambda e: e.tensor_copy(P["win"].ap[:, kk, n * 1024:(n + 1) * 1024], wi.ap[:]), r=[wi], w=[P["win"]])
            else:
                k.op("pool", lambda e: e.tensor_copy(P["win"].ap[:, kk, n * 1024:(n + 1) * 1024], wi.ap[:]), r=[wi], w=[P["win"]])
            q += 1


def emit_proj(k, L, xt, which, mj, dst, after_tr=None):
    P = L["P"]
    T_ = L["T"]
    pt = L["ps_t"]
    for c in range(8):
        k.op("pe", lambda e: e.transpose(pt[c // 4].ap[:, (c % 4) * 128:(c % 4 + 1) * 128], xt.ap[:, c * 128:(c + 1) * 128], P["identf"].ap[:]),
             r=[xt, P["identf"]], w=[pt[c // 4]])
    if after_tr is not None:
        after_tr()
    hT = T_["hT"]
    for c in range(8):
        k.op("dve", lambda e: e.tensor_scalar(hT.ap[:, c, :], pt[c // 4].ap[:, (c % 4) * 128:(c % 4 + 1) * 128],
                                              P["modF"].ap[:, 8 + c, mj:mj + 1], P["modF"].ap[:, c, mj:mj + 1], ALU.mult, ALU.add),
             r=[pt[c // 4], P["modF"]], w=[hT])
    pp = L["ps_p"]
    win = P["win"]
    for letter in which:
        if letter in "qk":
            base = 0 if letter == "q" else 1024
            res, ap = dst[letter]
            for m in range(8):
                pb = pp[m // 4]
                for kk in range(8):
                    k.op("pe", lambda e: e.matmul(pb.ap[:, (m % 4) * 128:(m % 4 + 1) * 128], win.ap[:, kk, base + m * 128:base + (m + 1) * 128], hT.ap[:, kk, :],
                                                  start=(kk == 0), stop=(kk == 7)), r=[win, hT], w=[pb])
            for half in range(2):
                sc_ = 0.125 if letter == "q" else 1.0
                k.op("dve", lambda e: e.tensor_scalar(ap[:, half * 4:(half + 1) * 4, :], pp[half].ap[:].rearrange("p (a b) -> p a b", a=4), sc_, None, ALU.mult),
                     r=[pp[half]], w=[res])
        else:
            base = 2048 if letter == "v" else 3072
            res, ap = dst[letter]
            for n in range(2):
                pb = pp[n]
                for kk in range(8):
                    k.op("pe", lambda e: e.matmul(pb.ap[:], hT.ap[:, kk, :], win.ap[:, kk, base + n * 512:base + (n + 1) * 512],
                                                  start=(kk == 0), stop=(kk == 7)), r=[win, hT], w=[pb])
                if letter == "v":
                    k.op("dve", lambda e: e.tensor_copy(ap[:, n * 8:(n + 1) * 8, 0:64], pb.ap[:].rearrange("p (a b) -> p a b", a=8)), r=[pb], w=[res])
                else:
                    zt = T_["ztmp"]
                    k.op("dve", lambda e: e.tensor_copy(zt.ap[:], pb.ap[:]), r=[pb], w=[zt])
                    k.op("act", lambda e: e.activation(ap[:, n * 512:(n + 1) * 512], zt.ap[:], AF.Silu), r=[zt], w=[res])


def emit_attn(k, L, qres, keys, bias_rows, szres, xres, woutg, out_dram_ap, out_res, is_output):
    P = L["P"]
    T_ = L["T"]
    ps_s = L["ps_s"]
    ps_o = L["ps_o"]
    PT = T_["PT"]
    O = T_["O"]
    nk = len(keys)
    rec = T_["rec"]
    sbanks = [ps_s, L["ps_p"]]

    def stage_a(h):
        c = h // 2
        pb = 64 * (h % 2)
        bk = sbanks[h % 2]
        for i, (kres, kap, vres, vap, kt) in enumerate(keys):
            bank = bk[i // 4]
            col = (i % 4) * 128
            has_bias = (kt is not None) and not L.get("nobias", False)
            if not has_bias:
                k.op("pe", lambda e: e.matmul(bank.ap[:, col:col + 128], kap[pb:pb + 64, c, :], qres.ap[pb:pb + 64, c, :],
                                              start=True, stop=True), r=[kres, qres], w=[bank])
            else:
                tb = P["TB"]
                hb = pb
                for rp in range(2):
                    s0 = na_slot(bias_rows[rp], kt)
                    cc = col + 64 * rp
                    k.op("pe", lambda e: e.matmul(bank.ap[:, cc:cc + 64], kap[pb:pb + 64, c, :], qres.ap[pb:pb + 64, c, 64 * rp:64 * rp + 64],
                                                  start=True, stop=False), r=[kres, qres], w=[bank])
                    k.op("pe", lambda e: e.matmul(bank.ap[:, cc:cc + 64],
                                                  tb.ap[hb:hb + 64, h // 2, s0:s0 + 2, :].rearrange("p a b -> p (a b)"),
                                                  P["identb"].ap[hb:hb + 64, hb:hb + 64], start=False, stop=True),
                         r=[tb, P["identb"]], w=[bank])

    def stage_b(h):
        bk = sbanks[h % 2]
        pt_ = PT[h % 2]
        stp = T_["stmp"][h % 2]
        n0 = min(nk, 4) * 128
        k.op("dve", lambda e: e.tensor_copy(stp.ap[:, 0:n0], bk[0].ap[:, 0:n0]), r=[bk[0]], w=[stp])
        if nk > 4:
            n1 = (nk - 4) * 128
            k.op("dve", lambda e: e.tensor_copy(stp.ap[:, 512:512 + n1], bk[1].ap[:, 0:n1]), r=[bk[1]], w=[stp])
        k.op("act", lambda e: e.activation(pt_.ap[:, 0:nk * 128], stp.ap[:, 0:nk * 128], AF.Exp), r=[stp], w=[pt_])

    def stage_c(h):
        pt_ = PT[h % 2]
        po = ps_o[(h // 4) % 2]
        oc = (h % 4) * 128
        for i, (kres, kap, vres, vap, kt) in enumerate(keys):
            k.op("pe", lambda e: e.matmul(po.ap[:, oc:oc + 65], pt_.ap[:, i * 128:(i + 1) * 128], vap[:, h, :],
                                          start=(i == 0), stop=(i == nk - 1)), r=[pt_, vres], w=[po])
        if h % 4 == 3:
            k.op("dve", lambda e: e.reciprocal(rec.ap[:], po.ap[:].rearrange("p (a b) -> p a b", a=4)[:, :, 64]), r=[po], w=[rec])
            for q in range(4):
                hh = h - 3 + q
                k.op("dve", lambda e: e.tensor_scalar(O.ap[:, hh * 64:(hh + 1) * 64], po.ap[:, q * 128:q * 128 + 64], rec.ap[:, q:q + 1], None, ALU.mult),
                     r=[po, rec], w=[O])

    stage_a(0)
    for h in range(H):
        stage_b(h)
        if h + 1 < H:
            stage_a(h + 1)
        stage_c(h)
    k.op("pool", lambda e: e.tensor_tensor(O.ap[:], O.ap[:], szres.ap[:], ALU.mult), r=[szres], w=[O])
    pt = L["ps_t"]
    for c in range(8):
        k.op("pe", lambda e: e.transpose(pt[c // 4].ap[:, (c % 4) * 128:(c % 4 + 1) * 128], O.ap[:, c * 128:(c + 1) * 128], P["identf"].ap[:]),
             r=[O, P["identf"]], w=[pt[c // 4]])
    GT = T_["GT"]
    for half in range(2):
        k.op("dve", lambda e: e.tensor_copy(GT.ap[:, half * 4:(half + 1) * 4, :], pt[half].ap[:].rearrange("p (a b) -> p a b", a=4)), r=[pt[half]], w=[GT])
    pp = L["ps_p"]
    for n in range(2):
        for kk in range(8):
            k.op("pe", lambda e: e.matmul(pp[n].ap[:], GT.ap[:, kk, :], woutg.ap[:, kk, n * 512:(n + 1) * 512], start=(kk == 0), stop=(kk == 7)),
                 r=[GT, woutg], w=[pp[n]])
    emit_resid_ln(k, L, xres, pp, out_dram_ap, out_res, is_output)


def emit_resid_ln(k, L, xres, pp, out_dram_ap, out_res, is_output):
    P = L["P"]
    T_ = L["T"]
    for n in range(2):
        k.op("dve", lambda e: e.scalar_tensor_tensor(xres.ap[:, n * 512:(n + 1) * 512], xres.ap[:, n * 512:(n + 1) * 512], float(ALPHA), pp[n].ap[:], ALU.mult, ALU.add),
             r=[pp[n]], w=[xres])
    st = T_["st"]
    mv = T_["mv"]
    for n in range(2):
        k.op("dve", lambda e: e.bn_stats(st.ap[:, n, :], xres.ap[:, n * 512:(n + 1) * 512]), r=[xres], w=[st])
    k.op("dve", lambda e: e.bn_aggr(mv.ap[:], st.ap[:].rearrange("p a b -> p (a b)")), r=[st], w=[mv])
    import os
    if os.environ.get("HYDBG") != "nosqrt":
        if "dummy" in T_:
            k.op("act", lambda e: e.activation(T_["dummy"].ap[:], P["eps"].ap[:], AF.Exp), r=[P["eps"]], w=[T_["dummy"]])
        k.op("act", lambda e: e.activation(mv.ap[:, 1:2], mv.ap[:, 1:2], AF.Sqrt, bias=P["eps"].ap[:, 0:1]), r=[P["eps"]], w=[mv])
    k.op("dve", lambda e: e.reciprocal(mv.ap[:, 1:2], mv.ap[:, 1:2]), w=[mv])
    k.op("dve", lambda e: e.tensor_scalar(xres.ap[:], xres.ap[:], mv.ap[:, 0:1], mv.ap[:, 1:2], ALU.subtract, ALU.mult), r=[mv], w=[xres])
    k.op("pool", lambda e: e.tensor_tensor(xres.ap[:], xres.ap[:], P["g_bc"].ap[:], ALU.mult), r=[P["g_bc"]], w=[xres])
    k.op("pool", lambda e: e.tensor_tensor(xres.ap[:], xres.ap[:], P["b_bc"].ap[:], ALU.add), r=[P["b_bc"]], w=[xres])
    k.dma("sp", out_dram_ap, xres.ap[:], r=[xres], w=[out_res] if out_res is not None else [], sem_res=xres, is_output=is_output)


def build_na_layer(with_ctx_out, stop_after=None, ntiles=NT, env=None):
    if env is None:
        nc = bass.Bass("TRN2", target_bir_lowering=False)
        dt_in = lambda n, s: nc.dram_tensor(n, list(s), F32, kind="ExternalInput").ap()
    else:
        nc = env["nc"]
        dt_in = lambda n, s: env["aps"][n]
    L = {}
    import os
    L["nobias"] = os.environ.get("NOBIAS") == "1"
    x = dt_in("x", [S, D]); ctx = dt_in("ctx", [CTX, D])
    L["c"] = dt_in("c", [D]); L["c_ctx"] = dt_in("c_ctx", [D])
    L["w_ada"] = dt_in("w_ada", [D, 3 * D]); L["b_ada"] = dt_in("b_ada", [3 * D])
    L["w_in"] = dt_in("w_in", [D, 4 * D]); L["w_out"] = dt_in("w_out", [D, D])
    L["ln_g"] = dt_in("ln_g", [D]); L["ln_b"] = dt_in("ln_b", [D])
    tbd = dt_in("tb_d", [128, 8, NSLOT * 64])
    idd = dt_in("identf_d", [128, 128])
    if env is None:
        x_out = nc.dram_tensor("x_out", [S, D], F32, kind="ExternalOutput").ap()
        ctx_out = nc.dram_tensor("ctx_out", [CTX, D], F32, kind="ExternalOutput").ap()
        k = KB(nc)
    else:
        x_out = env["aps"]["x_out"]; ctx_out = env["aps"].get("ctx_out")
        k = env["k"]
    P = {}
    L["P"] = P
    P["identf"] = k.sb("identf", [128, 128], F32)
    P["identb"] = k.sb("identb", [128, 128], BF16)
    P["eps"] = k.sb("eps", [128, 1], F32)
    P["modF"] = k.sb("modF", [128, 24, 2], F32)
    P["g_bc"] = k.sb("g_bc", [128, D], F32)
    P["b_bc"] = k.sb("b_bc", [128, D], F32)
    P["woutg"] = k.sb("woutg", [128, 8, D], BF16)
    P["woutgc"] = k.sb("woutgc", [128, 8, D], BF16)
    P["win"] = k.sb("win", [128, 8, 4 * D], BF16)
    P["TB"] = k.sb("TB", [128, 8, NSLOT, 64], BF16)
    psum = env["psum"] if env is not None else [k.ps("psb%d" % i_, [128, 512], F32) for i_ in range(8)]
    L["psum"] = psum
    L["ps_t"] = psum[0:2]; L["ps_p"] = psum[2:4]; L["ps_s"] = psum[4:6]; L["ps_o"] = psum[6:8]
    k.dma("sp", P["identf"].ap[:], idd, w=[P["identf"]])
    k.op("dve", lambda e: e.tensor_copy(P["identb"].ap[:], P["identf"].ap[:]), r=[P["identf"]], w=[P["identb"]])
    k.op("dve", lambda e: e.memset(P["eps"].ap[:], LN_EPS), w=[P["eps"]])
    T_ = {}
    L["T"] = T_
    T_["hT"] = k.sb("hT", [128, 8, 128], BF16)
    T_["PT"] = [k.sb("PT%d" % i_, [128, 896], BF16) for i_ in range(2)]
    T_["O"] = k.sb("O", [128, D], F32)
    T_["GT"] = k.sb("GT", [128, 8, 128], BF16)
    T_["rec"] = k.sb("rec", [128, 4], F32)
    T_["ztmp"] = k.sb("ztmp", [128, 512], F32)
    T_["stmp"] = [k.sb("stmp%d" % i_, [128, 896], F32) for i_ in range(2)]
    T_["st"] = k.sb("st", [128, 2, 6], F32)
    T_["mv"] = k.sb("mv", [128, 2], F32)
    KcT = [k.sb("KcT%d" % i_, [128, 8, 128], BF16) for i_ in range(2)]
    Vc = [k.sb("Vc%d" % i_, [128, H, 65], BF16) for i_ in range(2)]
    for v_ in Vc:
        k.op("pool", lambda e: e.memset(v_.ap[:, :, 64:65], 1.0), w=[v_])
    from contextlib import ExitStack
    with ExitStack() as st1:
        k.stack = st1
        tbs = [k.sbt("tbs%d" % i_, [128, NSLOT * 64], F32) for i_ in range(2)]
        for hh in range(8):
            k.dma("sp", tbs[hh % 2].ap[:], tbd[:, hh, :], w=[tbs[hh % 2]])
            k.op("dve", lambda e: e.tensor_copy(P["TB"].ap[:, hh, :, :].rearrange("p a b -> p (a b)"), tbs[hh % 2].ap[:]), r=[tbs[hh % 2]], w=[P["TB"]])
        emit_prep(k, nc, L, 0, with_ctx_out)
        kb_barrier(k)
    if stop_after == 'prep':
        k.finish(); return nc, k
    with ExitStack() as st2:
        k.stack = st2
        QcT = [k.sbt("QcT%d" % i_, [128, 8, 128], BF16) for i_ in range(2)]
        SZc = [k.sbt("SZc%d" % i_, [128, D], BF16) for i_ in range(2)]
        xc = [k.sbt("xc%d" % i_, [128, D], F32) for i_ in range(2)]
        for t in range(2):
            k.dma("sp", xc[t].ap[:], ctx[t * 128:(t + 1) * 128, :], w=[xc[t]])
            dst = {"k": (KcT[t], KcT[t].ap), "v": (Vc[t], Vc[t].ap), "q": (QcT[t], QcT[t].ap), "z": (SZc[t], SZc[t].ap)}
            emit_proj(k, L, xc[t], "qkvz" if with_ctx_out else "kv", 1, dst)
        ckeys = [(KcT[t], KcT[t].ap, Vc[t], Vc[t].ap, None) for t in range(2)]
        if with_ctx_out:
            for t in range(2):
                emit_attn(k, L, QcT[t], ckeys, None, SZc[t], xc[t], P["woutgc"], ctx_out[t * 128:(t + 1) * 128, :], None, True)
        kb_barrier(k)
    if stop_after == 'ctx':
        k.finish(); return nc, k
    xa = [k.sb("xa%d" % i_, [128, D], F32) for i_ in range(1)]
    xr = [k.sb("xr%d" % i_, [128, D], F32) for i_ in range(2)]
    NQ = 4
    NR = 6
    QT = [k.sb("QT%d" % i_, [128, 8, 128], BF16) for i_ in range(NQ)]
    SZ = [k.sb("SZ%d" % i_, [128, D], BF16) for i_ in range(NQ)]
    KTr = [k.sb("KTr%d" % i_, [128, 8, 128], BF16) for i_ in range(NR)]
    Vr = [k.sb("Vr%d" % i_, [128, H, 65], BF16) for i_ in range(NR)]
    for v_ in Vr:
        k.op("pool", lambda e: e.memset(v_.ap[:, :, 64:65], 1.0), w=[v_])
    done = -1
    xa_holds = [-1]
    for T in range(ntiles):
        kts = na_key_tiles(T)
        while done < max(kts):
            done += 1
            xt = xa[0]
            if xa_holds[0] != done:
                k.dma("sp", xt.ap[:], x[done * 128:(done + 1) * 128, :], w=[xt])
                xa_holds[0] = done
            dst = {"q": (QT[done % NQ], QT[done % NQ].ap), "k": (KTr[done % NR], KTr[done % NR].ap),
                   "v": (Vr[done % NR], Vr[done % NR].ap), "z": (SZ[done % NQ], SZ[done % NQ].ap)}

            def prefetch(nxt=done + 1):
                if nxt < NT:
                    k.dma("sp", xa[0].ap[:], x[nxt * 128:(nxt + 1) * 128, :], w=[xa[0]])
                    xa_holds[0] = nxt
            emit_proj(k, L, xt, "qkvz", 0, dst, after_tr=prefetch)
        xres = xr[T % 2]
        k.dma("sp", xres.ap[:], x[T * 128:(T + 1) * 128, :], w=[xres])
        keys = [(KTr[kt % NR], KTr[kt % NR].ap, Vr[kt % NR], Vr[kt % NR].ap, kt) for kt in kts] + ckeys
        emit_attn(k, L, QT[T % NQ], keys, (2 * T, 2 * T + 1), SZ[T % NQ], xres, P["woutg"], x_out[T * 128:(T + 1) * 128, :], None, True)
    if env is not None:
        return nc, k
    k.finish()
    return nc, k


def na_table_host(rpb_j):
    flat = np.concatenate([np.asarray(rpb_j, np.float32).ravel(), np.array([NEG], np.float32)])
    tb = flat[na_bias_index()]
    tb = tb.reshape(64, 8, 2, NSLOT, 64).transpose(2, 0, 1, 3, 4).reshape(128, 8, NSLOT * 64)
    return np.ascontiguousarray(tb)


HY_N = 8192
HY_EMB = 33


def hy_tables(J=32):
    N = 256 * J
    Lh = 128 * J
    p = np.arange(128, dtype=np.float64)[:, None]
    k1 = np.arange(128, dtype=np.float64)[None, :]
    G = np.zeros((128, J, 2, 128), np.float32)
    G2 = np.zeros((128, J, 2, 128), np.float32)
    GT = np.zeros((128, J, 2, 128), np.float32)
    for j in range(J):
        th = 2 * np.pi * (J * p + j) * (k1 + 0.5) / N
        G[:, j, 0, :] = np.cos(th); G[:, j, 1, :] = -np.sin(th)
        th2 = 2 * np.pi * (J * (p + 128) + j) * (k1 + 0.5) / N
        G2[:, j, 0, :] = -np.cos(th2); G2[:, j, 1, :] = np.sin(th2)
        GT[:, j, 0, :] = (2.0 / N) * np.cos(th).T
        GT[:, j, 1, :] = (2.0 / N) * (-np.sin(th)).T
    jj = np.arange(J)[:, None]; k2 = np.arange(J)[None, :]
    Wr = np.cos(2 * np.pi * jj * k2 / J); Wi = -np.sin(2 * np.pi * jj * k2 / J)
    SA = np.zeros((2, 2, J, 2, 2, J)); SB = np.zeros_like(SA); SHA = np.zeros_like(SA); SHB = np.zeros_like(SA)
    SI = np.zeros((2, 2, J, 2, 2, J))
    for hh in range(2):
        for dup in range(2):
            SA[0, hh, :, dup, hh, :] = Wr; SA[1, hh, :, dup, hh, :] = -Wi
            SB[0, hh, :, dup, hh, :] = Wi; SB[1, hh, :, dup, hh, :] = Wr
        SHA[:, hh, :, 0, hh, :] = SA[:, hh, :, 0, hh, :]; SHA[:, hh, :, 1, hh, :] = SB[:, hh, :, 1, hh, :]
        SHB[:, hh, :, 0, hh, :] = -SB[:, hh, :, 0, hh, :]; SHB[:, hh, :, 1, hh, :] = SA[:, hh, :, 1, hh, :]
        SI[0, hh, :, 0, hh, :] = Wr.T; SI[1, hh, :, 0, hh, :] = Wi.T
        SI[0, hh, :, 1, hh, :] = -Wi.T; SI[1, hh, :, 1, hh, :] = Wr.T
    S5 = np.zeros((128, 5, 128), np.float32)
    for i_, m in enumerate((SA, SB, SHA, SHB, SI)):
        S5[:4 * J, i_, :4 * J] = m.reshape(4 * J, 4 * J)
    pos = np.zeros((128, 2 * J), np.int64)
    for j in range(J):
        pos[:, 2 * j] = J * np.arange(128) + j
        pos[:, 2 * j + 1] = Lh - J * np.arange(128) - j
    valid = pos <= Lh - 1
    posc = np.where(valid, pos, 0)
    t_all = np.linspace(0.0, 1.0, Lh, dtype=np.float32)
    wpos = (np.float32(2.0 * np.pi) * np.arange(Lh, dtype=np.float32) / np.float32(Lh)).astype(np.float32)
    f = np.linspace(1e-4, 15.0, 16, dtype=np.float32)
    z_all = np.concatenate([t_all[:, None], np.cos(f[None, :] * wpos[:, None]), -np.sin(f[None, :] * wpos[:, None])], axis=1).astype(np.float32)
    zT = np.ascontiguousarray(z_all[posc].transpose(2, 1, 0)).astype(np.float32)
    negt = (-t_all[posc] * valid).astype(np.float32)
    sgn = np.where(valid, 1.0, 0.0).astype(np.float32)
    mx = np.log(1e-2) / 0.3; mn = np.log(1e-2) / 1.5
    deltas = np.abs(np.linspace(mn, mx, D, dtype=np.float32)).astype(np.float32)
    mask = np.ones((128, 3), np.float32)
    return {"hymask": mask, "hyG": G, "hyG2": G2, "hyGT": GT, "hyS5": S5, "hyzT": zT.reshape(33, 2 * J * 128), "hynegt": negt, "hysgn": sgn, "hydelta": deltas}


def _bank_rr(L):
    st = {"i": 0}

    def nxt():
        b = L["psum"][st["i"] % 8]
        st["i"] += 1
        return b
    return nxt


KC = 4


def load_tab_bf16(k, dram_ap, dst, nm):
    shp = dst.ap.shape
    n_outer = shp[1]
    inner = int(np.prod(shp[2:]))
    sts = [k.sbt(nm + "_st%d" % i_, [128, inner], F32) for i_ in range(2)]
    for i_ in range(n_outer):
        st = sts[i_ % 2]
        src = dram_ap[:, i_]
        dsta = dst.ap[:, i_]
        if len(shp) == 4:
            src = src.rearrange("p a b -> p (a b)")
            dsta = dsta.rearrange("p a b -> p (a b)")
        k.dma("sp", st.ap[:], src, w=[st])
        k.op("dve" if i_ % 2 == 0 else "pool", lambda e: e.tensor_copy(dsta, st.ap[:]), r=[st], w=[dst])


def emit_hy_filter(k, nc, L, Hd):
    from contextlib import ExitStack
    P = L["P"]
    J = L["J"]
    J4 = 4 * J
    dr = L["dram"]
    ps = L["psum"]
    stO = ExitStack()
    k.stack = stO
    rn = k.sbt("frn", [128, 2 * D], F32)
    stA = ExitStack()
    k.stack = stA
    Gs = k.sbt("fG", [128, J, 2, 128], BF16)
    load_tab_bf16(k, dr["hyG"], Gs, "fg1")
    ps_mlp = ps[0]; ps_hf = ps[1:3]; ps_nrm = ps[3:7]; ps_f1 = ps[7]
    w1 = k.sbt("fw1", [HY_EMB, 64], F32); w2 = k.sbt("fw2", [64, 64], F32); w3 = k.sbt("fw3", [64, 64], F32)
    w4 = k.sbt("fw4", [64, 4 * D], F32)
    fb = k.sbt("ffb", [64, 3, 2], F32)
    k.dma("sp", w1.ap[:], dr["hy_f_w1"], w=[w1]); k.dma("sp", w2.ap[:], dr["hy_f_w2"], w=[w2]); k.dma("sp", w3.ap[:], dr["hy_f_w3"], w=[w3])
    k.dma("sp", w4.ap[:], dr["hy_f_w4"], w=[w4])
    k.dma("sp", fb.ap[:, :, 0], dr["hy_f_freq"].rearrange("l f -> f l"), w=[fb], allow_slow_non_contiguous=True)
    for l_, nm in enumerate(["hy_f_b1", "hy_f_b2", "hy_f_b3"]):
        k.dma("sp", fb.ap[:, l_, 1:2], dr[nm].rearrange("(f o) -> f o", o=1), w=[fb], allow_slow_non_contiguous=True)
    k.op("dve", lambda e: e.tensor_tensor(fb.ap[:, :, 1], fb.ap[:, :, 1], fb.ap[:, :, 0], ALU.mult), w=[fb])
    zTs = [k.sbt("fzT%d" % i_, [HY_EMB, 512], F32) for i_ in range(2)]
    negt = k.sbt("fnegt", [128, 2 * J], F32); sgn = k.sbt("fsgn", [128, 2 * J], F32)
    k.dma("sp", negt.ap[:], dr["hynegt"], w=[negt]); k.dma("sp", sgn.ap[:], dr["hysgn"], w=[sgn])
    dbc = k.sbt("fdbc", [128, D], F32)
    k.dma("sp", dbc.ap[:], dr["hydelta"].partition_broadcast(128), w=[dbc])
    onesb = k.sbt("fones", [128, 128], BF16)
    k.op("dve", lambda e: e.memset(onesb.ap[:], 1.0), w=[onesb])
    G2 = k.sbt("fG2", [128, J, 2, 128], BF16)
    load_tab_bf16(k, dr["hyG2"], G2, "fg2")
    G = Gs
    arg = k.sbt("farg", [64, 512], F32); s2 = k.sbt("fs2", [64, 512], F32); s4 = k.sbt("fs4", [64, 512], F32)
    act_ = [k.sbt("fa%d" % i_, [64, 512], F32) for i_ in range(2)]
    dtmp = k.sbt("fdtmp", [128, D], F32); dec = k.sbt("fdec", [128, D], F32)
    taps = [[k.sbt("ftap%d_%d" % (b_, h_), [128, 2 * D], BF16) for h_ in range(2)] for b_ in range(1)]
    taps = [taps[0], taps[0]]
    absb = k.sbt("fabs", [128, 2 * D], BF16)
    afs = [k.sbt("fafs0", [128, 2, 2 * D], BF16)] * 2
    AFd = nc.dram_tensor("hyAFd" + L.get("sfx", ""), [2, 128, J, 2 * D], BF16, kind="Internal").ap()
    AFres = [k.dram_res("AFd%d" % j) for j in range(J)]
    for ch4 in range(max(1, (2 * J * 128) // 512)):
        zT = zTs[ch4 % 2]
        k.dma("sp", zT.ap[:], dr["hyzT"][:, ch4 * 512:(ch4 + 1) * 512], w=[zT])
        cur = zT.ap[:]
        srcs = [zT]
        for l_, w_ in enumerate([w1, w2, w3]):
            k.op("pe", lambda e: e.matmul(ps_mlp.ap[0:64, :], w_.ap[:], cur, start=True, stop=True), r=[w_] + srcs, w=[ps_mlp])
            k.op("dve", lambda e: e.tensor_scalar(arg.ap[:], ps_mlp.ap[0:64, :], fb.ap[:, l_, 0:1], fb.ap[:, l_, 1:2], ALU.mult, ALU.add), r=[ps_mlp, fb], w=[arg])
            k.op("act", lambda e: e.activation(s2.ap[:], arg.ap[:], AF.Sin, scale=0.5), r=[arg], w=[s2])
            k.op("act", lambda e: e.activation(s4.ap[:], arg.ap[:], AF.Sin, scale=0.25), r=[arg], w=[s4])
            k.op("dve", lambda e: e.tensor_tensor(s4.ap[:], s4.ap[:], s4.ap[:], ALU.mult), w=[s4])
            k.op("dve", lambda e: e.tensor_scalar(s4.ap[:], s4.ap[:], -2.0, 1.0, ALU.mult, ALU.add), w=[s4])
            a_ = act_[l_ % 2]
            k.op("dve", lambda e: e.scalar_tensor_tensor(a_.ap[:], s2.ap[:], 2.0, s4.ap[:], ALU.mult, ALU.mult), r=[s2, s4], w=[a_])
            cur = a_.ap[:]
            srcs = [a_]
        a3 = act_[0]
        for qi in range(4):
            q = ch4 * 4 + qi
            j, half = q // 2, q % 2
            tp = taps[j % 2][half]
            k.op("dve", lambda e: e.tensor_scalar(dtmp.ap[:], dbc.ap[:], negt.ap[:, q:q + 1], None, ALU.mult), r=[dbc, negt], w=[dtmp])
            k.op("act", lambda e: e.activation(dec.ap[:], dtmp.ap[:], AF.Exp), r=[dtmp], w=[dec])
            for o in range(2):
                for ch in range(2):
                    col = o * 2 * D + half * D + ch * 512
                    k.op("pe", lambda e: e.matmul(ps_hf[ch].ap[:], a3.ap[:, qi * 128:(qi + 1) * 128], w4.ap[:, col:col + 512], start=True, stop=True),
                         r=[a3, w4], w=[ps_hf[ch]])
                    k.op("dve", lambda e: e.scalar_tensor_tensor(tp.ap[:, o * D + ch * 512:o * D + (ch + 1) * 512], ps_hf[ch].ap[:], sgn.ap[:, q:q + 1],
                                                                 dec.ap[:, ch * 512:(ch + 1) * 512], ALU.mult, ALU.mult), r=[ps_hf[ch], sgn, dec], w=[tp])
            k.op("dve", lambda e: e.scalar_tensor_tensor(absb.ap[:], tp.ap[:], -1.0, tp.ap[:], ALU.mult, ALU.max), r=[tp], w=[absb])
            for cc in range(4):
                k.op("pe", lambda e: e.matmul(ps_nrm[cc].ap[:], onesb.ap[:], absb.ap[:, cc * 512:(cc + 1) * 512], start=(q == 0), stop=(q == 2 * J - 1)),
                     r=[onesb, absb], w=[ps_nrm[cc]])
            if half == 1:
                af = afs[j % 2]
                for ri in range(2):
                    for cc in range(4):
                        k.op("pe", lambda e: e.matmul(ps_f1.ap[:], G.ap[:, j, ri, :], taps[j % 2][0].ap[:, cc * 512:(cc + 1) * 512], start=True, stop=False),
                             r=[G, taps[j % 2][0]], w=[ps_f1])
                        k.op("pe", lambda e: e.matmul(ps_f1.ap[:], G2.ap[:, j, ri, :], taps[j % 2][1].ap[:, cc * 512:(cc + 1) * 512], start=False, stop=True),
                             r=[G2, taps[j % 2][1]], w=[ps_f1])
                        k.op("dve", lambda e: e.tensor_copy(af.ap[:, ri, cc * 512:(cc + 1) * 512], ps_f1.ap[:]), r=[ps_f1], w=[af])
                    k.dma("sp", AFd[ri, :, j, :], af.ap[:, ri, :], r=[af], w=[AFres[j]], sem_res=af)
    for cc in range(4):
        k.op("dve", lambda e: e.reciprocal(rn.ap[:, cc * 512:(cc + 1) * 512], ps_nrm[cc].ap[:]), r=[ps_nrm[cc]], w=[rn])
    kb_barrier(k)
    stA.close()
    stB = ExitStack()
    k.stack = stB
    S5 = k.sbt("fS5", [128, 5, 128], BF16)
    load_tab_bf16(k, dr["hyS5"], S5, "fs5")
    Bt = [k.sbt("fB%d" % i_, [128, KC, D], BF16) for i_ in range(2)]
    hst = [[k.sbt("fhs%d_%d" % (i_, ab), [128, KC, D], BF16) for ab in range(2)] for i_ in range(2)]
    it = 0
    for o in range(2):
        for q8 in range(64 // KC):
            B = Bt[it % 2]
            for ri in range(2):
                for hh in range(2):
                    pb = ri * 2 * J + hh * J
                    k.dma("sp", B.ap[pb:pb + J, :, :], AFd[ri, hh * 64 + q8 * KC:hh * 64 + q8 * KC + KC, :, o * D:(o + 1) * D].rearrange("k j c -> j k c"),
                          r=AFres, w=[B])
            for kl in range(KC):
                for ch in range(2):
                    for ab in range(2):
                        pb_ = ps[(ab * 2 + ch) % 8]
                        k.op("pe", lambda e: e.matmul(pb_.ap[0:J4, :], S5.ap[0:J4, 2 + ab, 0:J4], B.ap[0:J4, kl, ch * 512:(ch + 1) * 512], start=True, stop=True), r=[S5, B], w=[pb_])
                        k.op("dve", lambda e: e.tensor_tensor(hst[it % 2][ab].ap[0:J4, kl, ch * 512:(ch + 1) * 512], pb_.ap[0:J4, :], rn.ap[0:J4, o * D + ch * 512:o * D + (ch + 1) * 512], ALU.mult),
                             r=[pb_, rn], w=[hst[it % 2][ab]])
            for ab in range(2):
                k.dma("sp", Hd[o, ab, :, q8 * KC:(q8 + 1) * KC, :], hst[it % 2][ab].ap[0:J4], r=[hst[it % 2][ab]], w=[L["Hres"][o][ab][q8]], sem_res=hst[it % 2][ab])
            it += 1
    kb_barrier(k)
    stB.close()
    stO.close()


def emit_outproj_ln(k, L, O, xres, woutg, out_dram_ap, is_output=True):
    P = L["P"]
    T_ = L["T"]
    pt = L["ps_t"]
    for c in range(8):
        k.op("pe", lambda e: e.transpose(pt[c // 4].ap[:, (c % 4) * 128:(c % 4 + 1) * 128], O.ap[:, c * 128:(c + 1) * 128], P["identf"].ap[:]),
             r=[O, P["identf"]], w=[pt[c // 4]])
    GT = T_["GT"]
    for half in range(2):
        k.op("dve", lambda e: e.tensor_copy(GT.ap[:, half * 4:(half + 1) * 4, :], pt[half].ap[:].rearrange("p (a b) -> p a b", a=4)), r=[pt[half]], w=[GT])
    pp = L["ps_p"]
    for n in range(2):
        for kk in range(8):
            k.op("pe", lambda e: e.matmul(pp[n].ap[:], GT.ap[:, kk, :], woutg.ap[:, kk, n * 512:(n + 1) * 512], start=(kk == 0), stop=(kk == 7)),
                 r=[GT, woutg], w=[pp[n]])
    import os
    if os.environ.get("HYDBG") == "noln":
        for n in range(2):
            k.op("dve", lambda e: e.tensor_copy(xres.ap[:, n * 512:(n + 1) * 512], pp[n].ap[:]), r=[pp[n]], w=[xres])
        k.dma("sp", out_dram_ap, xres.ap[:], r=[xres], sem_res=xres, is_output=True)
        return
    emit_resid_ln(k, L, xres, pp, out_dram_ap, None, is_output)


def emit_hy_proj(k, nc, L, x, SC, mj=0):
    P = L["P"]
    dr = L["dram"]
    ps = L["psum"]
    pt = ps[0:2]
    J = L["J"]
    xv = x.rearrange("(p j) d -> j p d", j=J)
    win = P["win"]
    cw = k.sbt("hcw", [128, 3, 3 * D], F32)
    cb = k.sbt("hcb", [128, 3 * D], F32)
    for t in range(3):
        k.dma("sp", cw.ap[:, t, :], dr["hy_conv_w"][t, :].partition_broadcast(128), w=[cw])
    k.dma("sp", cb.ap[:], dr["hy_conv_b"].partition_broadcast(128), w=[cb])
    msk = k.sbt("hmask", [128, 3], F32)
    k.dma("sp", msk.ap[:], dr["hymask"], w=[msk])
    hTs = [k.sbt("hhT%d" % i_, [128, 8, 130], BF16) for i_ in range(2)]
    for h_ in hTs:
        k.op("pool", lambda e: e.memset(h_.ap[:], 0.0), w=[h_])
    xa = k.sbt("hxa", [128, D], F32)
    pre = [k.sbt("hpre%d" % i_, [128, 3 * D], F32) for i_ in range(3)]
    tmp = k.sbt("htmp", [128, 3 * D], F32)
    gt_ = k.sbt("hgt", [128, D], F32)
    sg = k.sbt("hsg", [128, D], F32)
    seq = [(J - 1, 0, -1)] + [(j, 1, j) for j in range(J)] + [(0, 2, J)]
    slot_of = {}
    k.dma("sp", xa.ap[:], xv[seq[0][0]], w=[xa])
    for it, (src, off, lj) in enumerate(seq):
        hT = hTs[it % 2]
        for c in range(8):
            k.op("pe", lambda e: e.transpose(pt[c // 4].ap[:, (c % 4) * 128:(c % 4 + 1) * 128], xa.ap[:, c * 128:(c + 1) * 128], P["identf"].ap[:]),
                 r=[xa, P["identf"]], w=[pt[c // 4]])
        if it + 1 < len(seq):
            k.dma("sp", xa.ap[:], xv[seq[it + 1][0]], w=[xa])
        for c in range(8):
            k.op("dve", lambda e: e.tensor_scalar(hT.ap[:, c, 1:129], pt[c // 4].ap[:, (c % 4) * 128:(c % 4 + 1) * 128],
                                                  P["modF"].ap[:, 8 + c, mj:mj + 1], P["modF"].ap[:, c, mj:mj + 1], ALU.mult, ALU.add),
                 r=[pt[c // 4], P["modF"]], w=[hT])
        pr = pre[it % 3]
        slot_of[lj] = pr
        ncol = 8 if off == 1 else 6
        for n in range(ncol):
            pb = ps[2 + n % 4]
            for kk in range(8):
                k.op("pe", lambda e: e.matmul(pb.ap[:], hT.ap[:, kk, off:off + 128], win.ap[:, kk, n * 512:(n + 1) * 512], start=(kk == 0), stop=(kk == 7)),
                     r=[hT, win], w=[pb])
            if n < 6:
                k.op("dve", lambda e: e.tensor_copy(pr.ap[:, n * 512:(n + 1) * 512], pb.ap[:]), r=[pb], w=[pr])
            else:
                k.op("dve", lambda e: e.tensor_copy(gt_.ap[:, (n - 6) * 512:(n - 5) * 512], pb.ap[:]), r=[pb], w=[gt_])
        if off == 1:
            k.op("act", lambda e: e.activation(sg.ap[:], gt_.ap[:], AF.Silu), r=[gt_], w=[sg])
            k.dma("sp", SC["sg"][lj], sg.ap[:], r=[sg], w=[SC["sg_res"][lj]], sem_res=sg)
        j = lj - 1
        if 0 <= j <= J - 1:
            a, b, c_ = slot_of[j - 1], slot_of[j], slot_of[j + 1]
            k.op("dve", lambda e: e.tensor_tensor(a.ap[:], a.ap[:], cw.ap[:, 0, :], ALU.mult), r=[cw], w=[a])
            k.op("pool", lambda e: e.tensor_tensor(tmp.ap[:], b.ap[:], cw.ap[:, 1, :], ALU.mult), r=[b, cw], w=[tmp])
            k.op("dve", lambda e: e.tensor_tensor(a.ap[:], a.ap[:], tmp.ap[:], ALU.add), r=[tmp], w=[a])
            k.op("pool", lambda e: e.tensor_tensor(tmp.ap[:], c_.ap[:], cw.ap[:, 2, :], ALU.mult), r=[c_, cw], w=[tmp])
            k.op("dve", lambda e: e.tensor_tensor(a.ap[:], a.ap[:], tmp.ap[:], ALU.add), r=[tmp], w=[a])
            k.op("pool", lambda e: e.tensor_tensor(a.ap[:], a.ap[:], cb.ap[:], ALU.add), r=[cb], w=[a])
            for gi, nm in enumerate(["v", "x1", "x2"]):
                k.dma("sp", SC[nm][j], a.ap[:, gi * D:(gi + 1) * D], r=[a], w=[SC[nm + "_res"][j]], sem_res=a)


def emit_hy_conv(k, nc, L, SC, o, Hd, epilogue, ep_alloc):
    from contextlib import ExitStack
    dr = L["dram"]
    J = L["J"]
    J4 = 4 * J
    ps = L["psum"]
    Ad = SC["Ad"]; Zd = SC["Zd"]
    NQ_ = 64 // KC
    with ExitStack() as st1:
        k.stack = st1
        G = k.sbt("cG", [128, J, 2, 128], BF16)
        load_tab_bf16(k, dr["hyG"], G, "cg")
        uf = k.sbt("cuf", [128, D], F32)
        ub = [k.sbt("cub%d" % i_, [128, D], BF16) for i_ in range(2)]
        As = [k.sbt("cAs%d" % i_, [128, 2, D], BF16) for i_ in range(2)]
        for j in range(J):
            k.dma("sp", uf.ap[:], SC["v"][j], r=[SC["v_res"][j]], w=[uf])
            u_ = ub[j % 2]
            k.op("pool", lambda e: e.tensor_copy(u_.ap[:], uf.ap[:]), r=[uf], w=[u_])
            a_ = As[j % 2]
            for ri in range(2):
                for ch in range(2):
                    pb = ps[(ri * 2 + ch) % 8]
                    k.op("pe", lambda e: e.matmul(pb.ap[:], G.ap[:, j, ri, :], u_.ap[:, ch * 512:(ch + 1) * 512], start=True, stop=True), r=[G, u_], w=[pb])
                    k.op("dve", lambda e: e.tensor_copy(a_.ap[:, ri, ch * 512:(ch + 1) * 512], pb.ap[:]), r=[pb], w=[a_])
                k.dma("sp", Ad[ri, :, j, :], a_.ap[:, ri, :], r=[a_], w=[SC["Ad_res"][j]], sem_res=a_)
        kb_barrier(k)
    with ExitStack() as st2:
        k.stack = st2
        S5 = k.sbt("cS5", [128, 5, 128], BF16)
        load_tab_bf16(k, dr["hyS5"], S5, "cs5")
        Bt = [k.sbt("cB%d" % i_, [128, KC, D], BF16) for i_ in range(2)]
        Ht = [[k.sbt("cH%d_%d" % (i_, ab), [128, KC, D], BF16) for ab in range(2)] for i_ in range(2)]
        Yt = k.sbt("cY", [128, KC, D], BF16)
        yts = [[k.sbt("cyt%d_%d" % (a_, b_), [128, 512], F32) for b_ in range(2)] for a_ in range(2)]
        Zs = [k.sbt("cZs%d" % i_, [128, KC, D], BF16) for i_ in range(2)]
        for q in range(NQ_):
            B = Bt[q % 2]
            for ri in range(2):
                for hh in range(2):
                    pb0 = ri * 2 * J + hh * J
                    k0 = hh * 64 + q * KC
                    k.dma("sp", B.ap[pb0:pb0 + J, :, :], Ad[ri, k0:k0 + KC, :, :].rearrange("k j c -> j k c"), r=SC["Ad_res"], w=[B])
            for ab in range(2):
                k.dma("sp", Ht[q % 2][ab].ap[0:J4], Hd[o, ab, :, q * KC:(q + 1) * KC, :], r=[L["Hres"][o][ab][q]], w=[Ht[q % 2][ab]])
            zs = Zs[q % 2]
            Htq = Ht[q % 2]
            its = [(kl, ch) for kl in range(KC) for ch in range(2)]

            def stage_s(i_):
                kl, ch = its[i_]
                sl = slice(ch * 512, (ch + 1) * 512)
                pa = ps[0 + i_ % 2]; pbk = ps[2 + i_ % 2]
                k.op("pe", lambda e: e.matmul(pa.ap[0:J4, :], S5.ap[0:J4, 0, 0:J4], B.ap[0:J4, kl, sl], start=True, stop=True), r=[S5, B], w=[pa])
                k.op("pe", lambda e: e.matmul(pbk.ap[0:J4, :], S5.ap[0:J4, 1, 0:J4], B.ap[0:J4, kl, sl], start=True, stop=True), r=[S5, B], w=[pbk])

            def stage_y(i_):
                kl, ch = its[i_]
                sl = slice(ch * 512, (ch + 1) * 512)
                pa = ps[0 + i_ % 2]; pbk = ps[2 + i_ % 2]
                y1 = yts[i_ % 2][0]; y2 = yts[i_ % 2][1]
                k.op("dve", lambda e: e.tensor_tensor(y1.ap[0:J4, :], pa.ap[0:J4, :], Htq[0].ap[0:J4, kl, sl], ALU.mult), r=[pa, Htq[0]], w=[y1])
                k.op("dve", lambda e: e.tensor_tensor(y2.ap[0:J4, :], pbk.ap[0:J4, :], Htq[1].ap[0:J4, kl, sl], ALU.mult), r=[pbk, Htq[1]], w=[y2])
                k.op("pool", lambda e: e.tensor_tensor(Yt.ap[0:J4, kl, sl], y1.ap[0:J4, :], y2.ap[0:J4, :], ALU.add), r=[y1, y2], w=[Yt])

            def stage_z(i_):
                kl, ch = its[i_]
                sl = slice(ch * 512, (ch + 1) * 512)
                pz = ps[4 + i_ % 2]
                k.op("pe", lambda e: e.matmul(pz.ap[0:J4, :], S5.ap[0:J4, 4, 0:J4], Yt.ap[0:J4, kl, sl], start=True, stop=True), r=[S5, Yt], w=[pz])
                k.op("dve", lambda e: e.tensor_copy(zs.ap[0:J4, kl, sl], pz.ap[0:J4, :]), r=[pz], w=[zs])

            stage_s(0)
            for i_ in range(len(its)):
                stage_y(i_)
                if i_ + 1 < len(its):
                    stage_s(i_ + 1)
                if i_ >= 1:
                    stage_z(i_ - 1)
            stage_z(len(its) - 1)
            for ri in range(2):
                for hh in range(2):
                    pb0 = ri * 2 * J + hh * J
                    k0 = hh * 64 + q * KC
                    k.dma("sp", Zd[ri, k0:k0 + KC, :, :].rearrange("k j c -> j k c"), zs.ap[pb0:pb0 + J, :, :], r=[zs], w=[SC["Zd_res"][q]], sem_res=zs)
        kb_barrier(k)
    with ExitStack() as st3:
        k.stack = st3
        GTt = k.sbt("cGT", [128, J, 2, 128], BF16)
        load_tab_bf16(k, dr["hyGT"], GTt, "cgt")
        Zin = [k.sbt("cZin%d" % i_, [128, 2, D], BF16) for i_ in range(2)]
        E = ep_alloc()
        for j in range(J):
            zi = Zin[j % 2]
            for ri in range(2):
                k.dma("sp", zi.ap[:, ri, :], Zd[ri, :, j, :], r=SC["Zd_res"], w=[zi])
            ybanks = [ps[4 + 2 * (j % 2)], ps[5 + 2 * (j % 2)]]
            for ch in range(2):
                for ri in range(2):
                    k.op("pe", lambda e: e.matmul(ybanks[ch].ap[:], GTt.ap[:, j, ri, :], zi.ap[:, ri, ch * 512:(ch + 1) * 512], start=(ri == 0), stop=(ri == 1)),
                         r=[GTt, zi], w=[ybanks[ch]])
            epilogue(j, ybanks, E)
        kb_barrier(k)


def build_hy_layer(ctx_mode, stop_after=None, env=None):
    from contextlib import ExitStack
    J = 2 if ctx_mode else 32
    SEQ = 128 * J
    if env is None:
        nc = bass.Bass("TRN2", target_bir_lowering=False)
        dt_in = lambda n, s_: nc.dram_tensor(n, list(s_), F32, kind="ExternalInput").ap()
    else:
        nc = env["nc"]
        dt_in = lambda n, s_: env["aps"][n]
    L = {"J": J}
    x = dt_in("x", [SEQ, D])
    L["c"] = dt_in("c", [D]); L["c_ctx"] = dt_in("c_ctx", [D])
    L["w_ada"] = dt_in("w_ada", [D, 3 * D]); L["b_ada"] = dt_in("b_ada", [3 * D])
    L["w_in"] = dt_in("w_in", [D, 4 * D]); L["w_out"] = dt_in("w_out", [D, D])
    L["ln_g"] = dt_in("ln_g", [D]); L["ln_b"] = dt_in("ln_b", [D])
    idd = dt_in("identf_d", [128, 128])
    dr = {}
    L["dram"] = dr
    for nm, shp in [("hy_conv_w", [3, 3 * D]), ("hy_conv_b", [3 * D]), ("hy_f_w1", [HY_EMB, 64]), ("hy_f_b1", [64]), ("hy_f_w2", [64, 64]),
                    ("hy_f_b2", [64]), ("hy_f_w3", [64, 64]), ("hy_f_b3", [64]), ("hy_f_w4", [64, 4 * D]), ("hy_f_freq", [3, 64]), ("hy_skip", [2, D]),
                    ("hyG", [128, J, 2, 128]), ("hyG2", [128, J, 2, 128]), ("hyGT", [128, J, 2, 128]), ("hyS5", [128, 5, 128]),
                    ("hymask", [128, 3]), ("hyzT", [HY_EMB, 2 * J * 128]), ("hynegt", [128, 2 * J]), ("hysgn", [128, 2 * J]), ("hydelta", [D])]:
        dr[nm] = dt_in(nm, shp)
    if env is None:
        x_out = nc.dram_tensor("x_out", [SEQ, D], F32, kind="ExternalOutput").ap()
        k = KB(nc)
    else:
        x_out = env["aps"]["x_out"]
        k = env["k"]
    P = {}
    L["P"] = P
    P["identf"] = k.sb("identf", [128, 128], F32)
    P["eps"] = k.sb("eps", [128, 1], F32)
    P["modF"] = k.sb("modF", [128, 24, 2], F32)
    P["g_bc"] = k.sb("g_bc", [128, D], F32)
    P["b_bc"] = k.sb("b_bc", [128, D], F32)
    P["woutg"] = k.sb("woutg", [128, 8, D], BF16)
    P["woutgc"] = P["woutg"]
    P["win"] = k.sb("win", [128, 8, 4 * D], BF16)
    psum = env["psum"] if env is not None else [k.ps("psb%d" % i_, [128, 512], F32) for i_ in range(8)]
    L["psum"] = psum
    L["ps_t"] = psum[0:2]; L["ps_p"] = psum[2:4]
    T_ = {}
    L["T"] = T_
    T_["GT"] = k.sb("GT", [128, 8, 128], BF16)
    T_["st"] = k.sb("st", [128, 2, 6], F32)
    T_["mv"] = k.sb("mv", [128, 2], F32)
    T_["dummy"] = k.sb("dummyact", [128, 1], F32)
    k.dma("sp", P["identf"].ap[:], idd, w=[P["identf"]])
    k.op("dve", lambda e: e.memset(P["eps"].ap[:], LN_EPS), w=[P["eps"]])
    sfx = env["sfx"] if env is not None else ""
    Hd = nc.dram_tensor("hyHd" + sfx, [2, 2, 4 * J, 64, D], BF16, kind="Internal").ap()
    L["Hres"] = [[[k.dram_res("H%d%d%d" % (o, ab, q)) for q in range(64 // KC)] for ab in range(2)] for o in range(2)]
    SC = {}
    for nm in ["v", "x1", "x2", "sg"]:
        t_ = nc.dram_tensor("hy_" + nm + sfx, [J, 128, D], F32, kind="Internal").ap()
        SC[nm] = [t_[j] for j in range(J)]
        SC[nm + "_res"] = [k.dram_res(nm + "%d" % j) for j in range(J)]
    SC["Ad"] = nc.dram_tensor("hyAd" + sfx, [2, 128, J, D], BF16, kind="Internal").ap()
    SC["Zd"] = nc.dram_tensor("hyZd" + sfx, [2, 128, J, D], BF16, kind="Internal").ap()
    SC["Ad_res"] = [k.dram_res("Ad%d" % j) for j in range(J)]
    SC["Zd_res"] = [k.dram_res("Zd%d" % q) for q in range(64 // KC)]
    xv = x.rearrange("(p j) d -> j p d", j=J)
    xov = x_out.rearrange("(p j) d -> j p d", j=J)
    L["sfx"] = sfx
    emit_hy_filter(k, nc, L, Hd)
    if stop_after == "filter":
        L["Hd"] = Hd
        k.finish(); return nc, k, L
    with ExitStack() as stp:
        k.stack = stp
        emit_prep(k, nc, L, 0, False, gate_sel=1 if ctx_mode else 0, nbuf=2)
        kb_barrier(k)
    with ExitStack() as stq:
        k.stack = stq
        emit_hy_proj(k, nc, L, x, SC, mj=1 if ctx_mode else 0)
        kb_barrier(k)
    if stop_after == "proj":
        k.finish(); return nc, k, L
    skb = k.sb("skipbc", [128, 2, D], F32)
    for o in range(2):
        k.dma("sp", skb.ap[:, o, :], dr["hy_skip"][o, :].partition_broadcast(128), w=[skb])

    def ep_alloc1():
        return {"vt": k.sbt("e_vt", [128, D], F32), "x1t": k.sbt("e_x1", [128, D], F32)}

    def epi1(j, yb, E):
        vt, x1t = E["vt"], E["x1t"]
        k.dma("sp", vt.ap[:], SC["v"][j], r=[SC["v_res"][j]], w=[vt])
        k.dma("sp", x1t.ap[:], SC["x1"][j], r=[SC["x1_res"][j]], w=[x1t])
        k.op("pool", lambda e: e.tensor_tensor(vt.ap[:], vt.ap[:], skb.ap[:, 0, :], ALU.mult), r=[skb], w=[vt])
        for ch in range(2):
            k.op("dve", lambda e: e.tensor_tensor(vt.ap[:, ch * 512:(ch + 1) * 512], vt.ap[:, ch * 512:(ch + 1) * 512], yb[ch].ap[:], ALU.add), r=[yb[ch]], w=[vt])
        k.op("pool", lambda e: e.tensor_tensor(vt.ap[:], vt.ap[:], x1t.ap[:], ALU.mult), r=[x1t], w=[vt])
        k.dma("sp", SC["v"][j], vt.ap[:], r=[vt], w=[SC["v_res"][j]], sem_res=vt)

    emit_hy_conv(k, nc, L, SC, 0, Hd, epi1, ep_alloc1)
    if stop_after == "conv1":
        k.finish(); return nc, k, L

    def ep_alloc2():
        return {"vt": k.sbt("e_vt", [128, D], F32), "x2t": k.sbt("e_x2", [128, D], F32), "sgt": k.sbt("e_sg", [128, D], F32),
                "xr": [k.sbt("e_xr%d" % i_, [128, D], F32) for i_ in range(2)]}

    def epi2(j, yb, E):
        vt, x2t, sgt = E["vt"], E["x2t"], E["sgt"]
        xr = E["xr"][j % 2]
        k.dma("sp", vt.ap[:], SC["v"][j], r=[SC["v_res"][j]], w=[vt])
        k.dma("sp", x2t.ap[:], SC["x2"][j], r=[SC["x2_res"][j]], w=[x2t])
        k.dma("sp", sgt.ap[:], SC["sg"][j], r=[SC["sg_res"][j]], w=[sgt])
        k.dma("sp", xr.ap[:], xv[j], w=[xr])
        k.op("pool", lambda e: e.tensor_tensor(vt.ap[:], vt.ap[:], skb.ap[:, 1, :], ALU.mult), r=[skb], w=[vt])
        for ch in range(2):
            k.op("dve", lambda e: e.tensor_tensor(vt.ap[:, ch * 512:(ch + 1) * 512], vt.ap[:, ch * 512:(ch + 1) * 512], yb[ch].ap[:], ALU.add), r=[yb[ch]], w=[vt])
        k.op("pool", lambda e: e.tensor_tensor(vt.ap[:], vt.ap[:], x2t.ap[:], ALU.mult), r=[x2t], w=[vt])
        k.op("pool", lambda e: e.tensor_tensor(vt.ap[:], vt.ap[:], sgt.ap[:], ALU.mult), r=[sgt], w=[vt])
        import os
        if os.environ.get("HYDBG") == "noproj":
            k.dma("sp", xov[j], vt.ap[:], r=[vt], sem_res=vt, is_output=True)
        else:
            emit_outproj_ln(k, L, vt, xr, P["woutg"], xov[j], True)

    emit_hy_conv(k, nc, L, SC, 1, Hd, epi2, ep_alloc2)
    if env is not None:
        return nc, k, L
    k.finish()
    return nc, k, L


_PROG = {}

_HY_W = ["hy_conv_w", "hy_conv_b", "hy_f_w1", "hy_f_b1", "hy_f_w2", "hy_f_b2", "hy_f_w3", "hy_f_b3", "hy_f_w4", "hy_f_freq", "hy_skip"]
_HY_W_SHAPES = {"hy_conv_w": [3, 3 * D], "hy_conv_b": [3 * D], "hy_f_w1": [HY_EMB, 64], "hy_f_b1": [64], "hy_f_w2": [64, 64], "hy_f_b2": [64],
                "hy_f_w3": [64, 64], "hy_f_b3": [64], "hy_f_w4": [64, 4 * D], "hy_f_freq": [3, 64], "hy_skip": [2, D]}
_HY_T_SHARED = {"hydelta": [D], "hyS5": [128, 5, 128], "hymask": [128, 3]}


def _hy_t_mode(J):
    return {"hyG": [128, J, 2, 128], "hyG2": [128, J, 2, 128], "hyGT": [128, J, 2, 128],
            "hyzT": [HY_EMB, 2 * J * 128], "hynegt": [128, 2 * J], "hysgn": [128, 2 * J]}


def build_fused():
    from contextlib import ExitStack
    nc = bass.Bass("TRN2", target_bir_lowering=False)
    din = lambda n, s_: nc.dram_tensor(n, list(s_), F32, kind="ExternalInput").ap()
    A = {}
    A["x"] = din("x", [S, D]); A["ctx"] = din("ctx", [CTX, D]); A["c"] = din("c", [D]); A["c_ctx"] = din("c_ctx", [D])
    A["w_ada"] = din("w_ada", [4, D, 3 * D]); A["b_ada"] = din("b_ada", [4, 3 * D]); A["w_in"] = din("w_in", [4, D, 4 * D])
    A["w_out"] = din("w_out", [4, D, D]); A["ln_g"] = din("ln_g", [4, D]); A["ln_b"] = din("ln_b", [4, D])
    A["tb_d"] = din("tb_d", [2, 128, 8, NSLOT * 64]); A["identf_d"] = din("identf_d", [128, 128])
    for nm in _HY_W:
        A[nm] = din(nm, [2] + _HY_W_SHAPES[nm])
    for nm, shp in _HY_T_SHARED.items():
        A[nm] = din(nm, shp)
        if nm != "hydelta":
            A[nm + "_c"] = din(nm + "_c", shp)
    for nm, shp in _hy_t_mode(32).items():
        A[nm] = din(nm, shp)
    for nm, shp in _hy_t_mode(2).items():
        A[nm + "_c"] = din(nm + "_c", shp)
    y = nc.dram_tensor("y_out", [S, D], F32, kind="ExternalOutput").ap()
    xs0 = nc.dram_tensor("xs0", [S, D], F32, kind="Internal").ap()
    xs1 = nc.dram_tensor("xs1", [S, D], F32, kind="Internal").ap()
    cs0 = nc.dram_tensor("cs0", [CTX, D], F32, kind="Internal").ap()
    cs1 = nc.dram_tensor("cs1", [CTX, D], F32, kind="Internal").ap()
    k = KB(nc)
    psum = [k.ps("psb%d" % i_, [128, 512], F32) for i_ in range(8)]

    def run_layer(fn):
        k.layer_stack = ExitStack()
        fn()
        kb_barrier(k)
        k.layer_stack.close()
        k.layer_stack = None
        k.recycle()

    def common(i):
        return {"c": A["c"], "c_ctx": A["c_ctx"], "w_ada": A["w_ada"][i], "b_ada": A["b_ada"][i], "w_in": A["w_in"][i], "w_out": A["w_out"][i],
                "ln_g": A["ln_g"][i], "ln_b": A["ln_b"][i], "identf_d": A["identf_d"]}

    def hy_aps(j, ctx_mode):
        d_ = {nm: A[nm][j] for nm in _HY_W}
        for nm in list(_HY_T_SHARED) + list(_hy_t_mode(2)):
            d_[nm] = A[nm + "_c"] if (ctx_mode and nm != "hydelta") else A[nm]
        return d_

    run_layer(lambda: build_na_layer(True, env={"nc": nc, "k": k, "psum": psum,
                                                 "aps": dict(common(0), x=A["x"], ctx=A["ctx"], tb_d=A["tb_d"][0], x_out=xs0, ctx_out=cs0)}))
    run_layer(lambda: build_hy_layer(False, env={"nc": nc, "k": k, "psum": psum, "sfx": "_a",
                                                  "aps": dict(common(1), **hy_aps(0, False), x=xs0, x_out=xs1)}))
    run_layer(lambda: build_hy_layer(True, env={"nc": nc, "k": k, "psum": psum, "sfx": "_b",
                                                 "aps": dict(common(1), **hy_aps(0, True), x=cs0, x_out=cs1)}))
    run_layer(lambda: build_na_layer(False, env={"nc": nc, "k": k, "psum": psum,
                                                  "aps": dict(common(2), x=xs1, ctx=cs1, tb_d=A["tb_d"][1], x_out=xs0)}))
    run_layer(lambda: build_hy_layer(False, env={"nc": nc, "k": k, "psum": psum, "sfx": "_c",
                                                  "aps": dict(common(3), **hy_aps(1, False), x=xs0, x_out=y)}))
    kb_barrier(k)
    return nc, k


def kernel(**inputs):
    inp = {k_: np.ascontiguousarray(np.asarray(v, dtype=np.float32)) for k_, v in inputs.items()}
    B = inp["x"].shape[0]
    if "fused" not in _PROG:
        _PROG["fused"] = build_fused()[0]
    shared = {nm: inp[nm] for nm in ["c_ctx", "w_ada", "b_ada", "w_in", "w_out", "ln_g", "ln_b"] + _HY_W}
    shared["tb_d"] = np.stack([na_table_host(inp["na_rpb"][0]), na_table_host(inp["na_rpb"][1])], axis=0)
    shared["identf_d"] = np.eye(128, dtype=np.float32)
    tl = hy_tables(32)
    tc = hy_tables(2)
    for nm in list(_HY_T_SHARED) + list(_hy_t_mode(2)):
        shared[nm] = tl[nm]
        if nm != "hydelta":
            shared[nm + "_c"] = tc[nm]
    in_maps = [dict(shared, x=inp["x"][b], ctx=inp["ctx"][b], c=inp["c"][b]) for b in range(B)]
    res = run_bass_kernel_spmd(_PROG["fused"], in_maps, core_ids=list(range(B)))
    return np.stack([res.results[b]["y_out"] for b in range(B)], axis=0).astype(np.float32)
```
